# Optimizing a Trainium2 kernel written in Bass

```python
import jax, jax.numpy as jnp
from jax import lax
import numpy as np

D_MODEL = 1024
BATCH = 8
SEQ = 2048
DEPTH = 2
DEC_BATCH = 128
DEC_SEQ = 1
PAST_LEN = 16384
PAGE_SIZE = 128

MIX_WIDTH = D_MODEL
HEAD_DIM = 64
W_A = D_MODEL // 4
W_B = 3 * D_MODEL // 8
W_C = MIX_WIDTH - W_A - W_B
POOL_WINDOWS = (2, 4, 8, 16)
N_POOL_GROUPS = 4
POOL_GROUP = W_A // N_POOL_GROUPS
POOL_HIST = max(POOL_WINDOWS) - 1
CONV_B_WIDTH = 31
CONV_C_WIDTH = 3
PLE_DIM = 256
EPS = 1e-6
IN_COLS = 2 * W_A + 3 * W_B + 4 * W_C

kernel_name = "hybrid_pool_conformer_shortconv_step"


def _rmsnorm(x, g):
    xf = x.astype(jnp.float32)
    y = xf * lax.rsqrt(jnp.mean(xf * xf, axis=-1, keepdims=True) + EPS)
    return (y * g.astype(jnp.float32)).astype(x.dtype)


def _layernorm(x, g, b):
    xf = x.astype(jnp.float32)
    mu = jnp.mean(xf, axis=-1, keepdims=True)
    xc = xf - mu
    var = jnp.mean(xc * xc, axis=-1, keepdims=True)
    y = xc * lax.rsqrt(var + EPS) * g.astype(jnp.float32) + b.astype(jnp.float32)
    return y.astype(x.dtype)


def _causal_dwconv(hist, u, w):
    k, c = w.shape
    z = jnp.concatenate([hist.astype(u.dtype), u], axis=1)
    out = lax.conv_general_dilated(
        z, w.astype(u.dtype)[:, None, :], window_strides=(1,), padding='VALID',
        dimension_numbers=('NWC', 'WIO', 'NWC'), feature_group_count=c)
    return out, z[:, z.shape[1] - (k - 1):]


def _pool_mixer(hist, v, pos0, w_mix, scale):
    bsz, t_len, _ = v.shape
    z = jnp.concatenate([hist.astype(v.dtype), v], axis=1)
    zf = z.astype(jnp.float32)
    cs = jnp.concatenate([jnp.zeros((bsz, 1, W_A), jnp.float32),
                          jnp.cumsum(zf, axis=1)], axis=1)
    t = jnp.arange(t_len)
    outs = []
    for g, w in enumerate(POOL_WINDOWS):
        csg = cs[:, :, g * POOL_GROUP:(g + 1) * POOL_GROUP]
        s = csg[:, POOL_HIST + 1:POOL_HIST + 1 + t_len] - csg[:, POOL_HIST + 1 - w:POOL_HIST + 1 - w + t_len]
        cnt = jnp.minimum(w, pos0 + t + 1).astype(jnp.float32)
        outs.append(s / cnt[None, :, None])
    pooled = jnp.concatenate(outs, axis=-1) - v.astype(jnp.float32)
    pooled = pooled.reshape(bsz, t_len, N_POOL_GROUPS, POOL_GROUP)
    mixed = jnp.einsum('btgc,gcd->btgd', pooled, w_mix.astype(jnp.float32))
    mixed = mixed.reshape(bsz, t_len, W_A) * scale.astype(jnp.float32)
    return mixed.astype(v.dtype), z[:, z.shape[1] - POOL_HIST:]


def _layer(x, pe, h_pool, h_conv, h_sconv, pos0, norm_g, w_in, w_pool_mix, pool_scale,
           conv_b_w, conv_b_b, ln_b_g, ln_b_b, sconv_w, w_out, w_ple, w_ple_gate):
    h = _rmsnorm(x, norm_g)
    proj = h @ w_in
    sizes = [W_A, W_A, W_B, W_B, W_B, W_C, W_C, W_C, W_C]
    idx = [int(i) for i in np.cumsum(sizes)[:-1]]
    v_a, z_a, a_b, g_b, z_b, x_c, b_c, c_c, z_c = jnp.split(proj, idx, axis=-1)
    y_a, n_pool = _pool_mixer(h_pool, v_a, pos0, w_pool_mix, pool_scale)
    y_a = y_a * jax.nn.silu(z_a)
    u_b = a_b * jax.nn.sigmoid(g_b)
    c_b, n_conv = _causal_dwconv(h_conv, u_b, conv_b_w)
    c_b = c_b + conv_b_b
    y_b = jax.nn.silu(_layernorm(c_b, ln_b_g, ln_b_b)) * jax.nn.silu(z_b)
    u_c = c_c * x_c
    s_c, n_sconv = _causal_dwconv(h_sconv, u_c, sconv_w)
    y_c = b_c * s_c * jax.nn.silu(z_c)
    y = jnp.concatenate([y_a, y_b, y_c], axis=-1) @ w_out
    x = x + y
    gate = jax.nn.sigmoid((x @ w_ple_gate).astype(jnp.float32)).astype(x.dtype)
    x = x + (pe @ w_ple) * gate
    return x, n_pool, n_conv, n_sconv


def setup_inputs(seed: int = 0) -> dict:
    key = jax.random.key(seed)
    ks = jax.random.split(key, 24)
    f32 = jnp.float32
    nrm = lambda k, s, sc: jax.random.normal(k, s, f32) * sc
    return {
        "x_prompt": nrm(ks[0], (BATCH, SEQ, D_MODEL), 1.0),
        "x_sample": nrm(ks[1], (DEC_BATCH, DEC_SEQ, D_MODEL), 1.0),
        "state_pool": nrm(ks[2], (DEPTH, DEC_BATCH, POOL_HIST, W_A), 1.0),
        "state_conv": nrm(ks[3], (DEPTH, DEC_BATCH, CONV_B_WIDTH - 1, W_B), 0.5),
        "state_sconv": nrm(ks[4], (DEPTH, DEC_BATCH, CONV_C_WIDTH - 1, W_C), 1.0),
        "p_prompt": nrm(ks[5], (DEPTH, BATCH, SEQ, PLE_DIM), 1.0),
        "p_sample": nrm(ks[6], (DEPTH, DEC_BATCH, DEC_SEQ, PLE_DIM), 1.0),
        "norm_g": 1.0 + nrm(ks[7], (DEPTH, D_MODEL), 0.02),
        "w_in": nrm(ks[8], (DEPTH, D_MODEL, IN_COLS), D_MODEL ** -0.5),
        "w_pool_mix": nrm(ks[9], (DEPTH, N_POOL_GROUPS, POOL_GROUP, POOL_GROUP), POOL_GROUP ** -0.5),
        "pool_scale": 1.0 + nrm(ks[10], (DEPTH, W_A), 0.02),
        "conv_b_w": nrm(ks[11], (DEPTH, CONV_B_WIDTH, W_B), CONV_B_WIDTH ** -0.5),
        "conv_b_b": nrm(ks[12], (DEPTH, W_B), 0.02),
        "ln_b_g": 1.0 + nrm(ks[13], (DEPTH, W_B), 0.02),
        "ln_b_b": nrm(ks[14], (DEPTH, W_B), 0.02),
        "sconv_w": nrm(ks[15], (DEPTH, CONV_C_WIDTH, W_C), CONV_C_WIDTH ** -0.5),
        "w_out": nrm(ks[16], (DEPTH, MIX_WIDTH, D_MODEL), MIX_WIDTH ** -0.5),
        "w_ple": nrm(ks[17], (DEPTH, PLE_DIM, D_MODEL), PLE_DIM ** -0.5),
        "w_ple_gate": nrm(ks[18], (DEPTH, D_MODEL, D_MODEL), D_MODEL ** -0.5),
        "final_norm_g": 1.0 + nrm(ks[19], (D_MODEL,), 0.02),
    }


def reference(x_prompt, x_sample, state_pool, state_conv, state_sconv, p_prompt, p_sample,
              norm_g, w_in, w_pool_mix, pool_scale, conv_b_w, conv_b_b, ln_b_g, ln_b_b,
              sconv_w, w_out, w_ple, w_ple_gate, final_norm_g):
    xp, xs = x_prompt, x_sample
    bp = x_prompt.shape[0]
    dt = x_prompt.dtype
    pool_p, pool_s, conv_p, conv_s, sconv_p, sconv_s = [], [], [], [], [], []
    for i in range(DEPTH):
        lw = (norm_g[i], w_in[i], w_pool_mix[i], pool_scale[i], conv_b_w[i], conv_b_b[i],
              ln_b_g[i], ln_b_b[i], sconv_w[i], w_out[i], w_ple[i], w_ple_gate[i])
        xp, a, b, c = _layer(
            xp, p_prompt[i],
            jnp.zeros((bp, POOL_HIST, W_A), dt),
            jnp.zeros((bp, CONV_B_WIDTH - 1, W_B), dt),
            jnp.zeros((bp, CONV_C_WIDTH - 1, W_C), dt),
            0, *lw)
        pool_p.append(a); conv_p.append(b); sconv_p.append(c)
        xs, a, b, c = _layer(xs, p_sample[i], state_pool[i], state_conv[i], state_sconv[i],
                             PAST_LEN, *lw)
        pool_s.append(a); conv_s.append(b); sconv_s.append(c)
    y_prompt = _rmsnorm(xp, final_norm_g)
    y_sample = _rmsnorm(xs, final_norm_g)
    return (y_prompt, y_sample,
            jnp.stack(pool_p), jnp.stack(pool_s),
            jnp.stack(conv_p), jnp.stack(conv_s),
            jnp.stack(sconv_p), jnp.stack(sconv_s))
```

```python
import contextlib
import numpy as np
import concourse.bass as bass
import concourse.mybir as mybir
from concourse.bass_utils import run_bass_kernel_spmd

F32 = mybir.dt.float32
BF16 = mybir.dt.bfloat16
AF = mybir.ActivationFunctionType
ALU = mybir.AluOpType

NCORES = 8
D = 1024
L = 2
SEQ = 2048
NS = 16
TT = SEQ + NS
TWS = [296] * 6 + [288]
NT = len(TWS)
TOFF = [sum(TWS[:i]) for i in range(NT)]
TWMAX = 296
W_A, W_B, W_C = 256, 384, 384
C_VA, C_ZA, C_AB, C_GB, C_ZB, C_XC, C_BC, C_CC, C_ZC = 0, 256, 512, 896, 1280, 1664, 2048, 2432, 2816
IN_COLS = 3200
EPS = 1e-6
WIN_GROUPS = [(C_GB, C_GB + 384), (C_AB, C_AB + 384), (C_VA, C_VA + 512), (C_XC, C_XC + 384), (C_CC, C_CC + 384),
              (C_ZC, C_ZC + 384), (C_BC, C_BC + 384), (C_ZB, C_ZB + 384)]
P_NG = 0
P_FG = P_NG + 16
P_PS = P_FG + 8
P_CW = P_PS + 4
P_CB = P_CW + 186
P_LG = P_CB + 6
P_LB = P_LG + 6
P_SW = P_LB + 6
NPRM = P_SW + 18
RING = 4
WARM_MM = 130
LAST_BLK = {}
for _i, (_g, _c) in enumerate([(0, C_GB), (0, C_GB + 128), (0, C_GB + 256), (1, C_AB), (1, C_AB + 128), (1, C_AB + 256),
                               (3, C_XC), (3, C_XC + 128), (3, C_XC + 256), (4, C_CC), (4, C_CC + 128), (4, C_CC + 256),
                               (2, C_VA), (2, C_VA + 128), (2, C_VA + 256), (2, C_VA + 384)]):
    LAST_BLK[_i] = (_g, _c)
RECIP_MODE = 0
B_RATIO = 1.0
OPT_BUILD = 'dve'
OPT_SKIPSELF = True


class Ev:
    __slots__ = ("sem", "val", "snap")

    def __init__(self, sem, val, snap):
        self.sem, self.val, self.snap = sem, val, snap


class Buf:
    __slots__ = ("name", "writer", "readers")

    def __init__(self, name):
        self.name, self.writer, self.readers = name, None, []


class Eng:
    def __init__(self, name, h, semname, is_pe=False):
        self.name, self.h, self.semname, self.is_pe = name, h, semname, is_pe
        self.count = 0
        self.seen = {}
        self._snap = None
        self.prog = []
        self.abs = []

    def snap(self):
        if self._snap is None:
            self._snap = dict(self.seen)
        return self._snap


class Tracker:
    def __init__(self):
        self.sems = {}
        self.semval = {}

    def add_sem(self, name, handle):
        self.sems[name] = handle
        self.semval[name] = 0

    def _waits(self, eng, raw, oth, skip_self=False):
        need = []
        for ev in raw:
            if ev.sem == eng.semname and eng.is_pe:
                continue
            need.append(ev)
        for ev in oth:
            if ev.sem == eng.semname and (eng.is_pe or skip_self):
                continue
            need.append(ev)
        need.sort(key=lambda e: -e.val)
        for ev in need:
            if eng.seen.get(ev.sem, 0) >= ev.val:
                continue
            eng.prog.append(lambda hh, s_=self.sems[ev.sem], v_=ev.val: hh.wait_ge(s_, v_))
            eng.abs.append(("wait", ev.sem, ev))
            eng.seen[ev.sem] = ev.val
            if ev.snap:
                for k, v in ev.snap.items():
                    if eng.seen.get(k, 0) < v:
                        eng.seen[k] = v
            eng._snap = None

    def op(self, eng, fn, reads=(), writes=(), final=True, skip_self=False):
        raw = [b.writer for b in reads if b.writer is not None]
        oth = []
        for b in reads:
            if b.name.startswith("bank"):
                oth.extend(r for r in b.readers if r.sem != eng.semname)
        for b in writes:
            if b.writer is not None:
                oth.append(b.writer)
            oth.extend(b.readers)
        self._waits(eng, raw, oth, skip_self and OPT_SKIPSELF)
        if final:
            eng.count += 1
            eng.prog.append(lambda hh, fn=fn, s_=self.sems[eng.semname]: fn(hh).then_inc(s_, 1))
            eng.abs.append(("inc", eng.semname, 1))
            val = eng.count
        else:
            eng.prog.append(lambda hh, fn=fn: fn(hh))
            val = eng.count + 1
        ev = Ev(eng.semname, val, eng.snap())
        for b in reads:
            b.readers.append(ev)
        for b in writes:
            b.writer = ev
            b.readers = []
        return None

    def dma(self, q, out_ap, in_ap, semname, reads=(), writes=(), group=None):
        raw = [b.writer for b in reads if b.writer is not None]
        oth = []
        for b in writes:
            if b.writer is not None:
                oth.append(b.writer)
            oth.extend(b.readers)
        self._waits(q, raw, oth)
        self.semval[semname] += 16
        q.prog.append(lambda hh, o_=out_ap, i_=in_ap, s_=self.sems[semname]: hh.dma_start(out=o_, in_=i_).then_inc(s_, 16))
        q.abs.append(("inc", semname, 16))
        ev = Ev(semname, self.semval[semname], q.snap())
        if group is not None:
            group.append(ev)
        for b in reads:
            b.readers.append(ev)
        for b in writes:
            b.writer = ev
            b.readers = []
        return ev

    def close_group(self, group, semname):
        tot = self.semval[semname]
        for ev in group:
            ev.val = tot


def check_deadlock(engs):
    val = {}
    pc = {e.name: 0 for e in engs}
    progress = True
    while progress:
        progress = False
        for e in engs:
            while pc[e.name] < len(e.abs):
                kind, sem, a = e.abs[pc[e.name]]
                if kind == "wait":
                    if val.get(sem, 0) >= a.val:
                        pc[e.name] += 1
                        progress = True
                    else:
                        break
                else:
                    val[sem] = val.get(sem, 0) + a
                    pc[e.name] += 1
                    progress = True
    stuck = {e.name: (pc[e.name], len(e.abs), e.abs[pc[e.name]][1], e.abs[pc[e.name]][2].val, val.get(e.abs[pc[e.name]][1], 0))
             for e in engs if pc[e.name] < len(e.abs)}
    assert not stuck, f"DEADLOCK in sync plan: {stuck}"


def build_nc():
    nc = bass.Bass("TRN2", target_bir_lowering=False)
    dt = nc.dram_tensor
    xT = dt("xT", [D, TT], F32, kind="ExternalInput").ap()
    peT = dt("peT", [L, 256, TT], F32, kind="ExternalInput").ap()
    w_in = dt("w_in", [L, D, IN_COLS], F32, kind="ExternalInput").ap()
    w_out = dt("w_out", [L, D, D], F32, kind="ExternalInput").ap()
    w_ple = dt("w_ple", [L, 256, D], F32, kind="ExternalInput").ap()
    w_gate = dt("w_gate", [L, D, D], F32, kind="ExternalInput").ap()
    w_pool = dt("w_pool", [L, 4, 64, 64], F32, kind="ExternalInput").ap()
    prm_d = dt("prm", [128, NPRM], F32, kind="ExternalInput").ap()
    st_pool_f = dt("st_pool_f", [L, 256, 15, NS], F32, kind="ExternalInput").ap()
    st_conv_f = dt("st_conv_f", [L, 384, 30, NS], F32, kind="ExternalInput").ap()
    st_sconv_f = dt("st_sconv_f", [L, 384, 2, NS], F32, kind="ExternalInput").ap()
    st_pool_n = dt("st_pool_n", [L, NS, 15, 256], F32, kind="ExternalInput").ap()
    st_conv_n = dt("st_conv_n", [L, NS, 30, 384], F32, kind="ExternalInput").ap()
    st_sconv_n = dt("st_sconv_n", [L, NS, 2, 384], F32, kind="ExternalInput").ap()
    yT = dt("yT", [D, TT], F32, kind="ExternalOutput").ap()
    o_pool_t = dt("o_pool_t", [L, 256, 31], F32, kind="ExternalOutput").ap()
    o_conv_t = dt("o_conv_t", [L, 384, 46], F32, kind="ExternalOutput").ap()
    o_sconv_t = dt("o_sconv_t", [L, 384, 18], F32, kind="ExternalOutput").ap()
    o_pool_old = dt("o_pool_old", [L, NS, 14, 256], F32, kind="ExternalOutput").ap()
    o_conv_old = dt("o_conv_old", [L, NS, 29, 384], F32, kind="ExternalOutput").ap()
    o_sconv_old = dt("o_sconv_old", [L, NS, 1, 384], F32, kind="ExternalOutput").ap()

    with contextlib.ExitStack() as es:
        def sb(name, shape, dtype):
            return es.enter_context(nc.sbuf_tensor(name, shape, dtype))

        x = sb("x", [128, 8, TT], F32)
        win = sb("win", [128, 8, IN_COLS], BF16)
        ring = sb("ring", [128, RING, 8, 128], BF16)
        wple = sb("wple", [128, 2, D], BF16)
        d31 = sb("d31", [128, 93, 128], BF16)
        d3 = sb("d3", [128, 9, 128], BF16)
        pm = sb("pm", [128, 6, 128], BF16)
        ones = sb("ones", [128, 128], BF16)
        ident = sb("ident", [128, 128], BF16)
        prm = sb("prm_s", [128, NPRM], F32)
        epsT = sb("epsT", [128, 1], F32)
        stage = sb("stage", [128, L, 2, 64], F32)
        rtab = sb("rtab", [128, 2, 15], F32)
        t15 = sb("t15", [128, 2, 15], F32)
        h = sb("h", [128, 2, 8, TWMAX], BF16)
        xb = sb("xb", [128, 8, TWMAX], BF16)
        ycat = sb("ycat", [128, 8, TWMAX], BF16)
        pe = sb("pe", [128, 2, 2, TWMAX], BF16)
        ub = sb("ub", [128, 3, 30 + TWMAX], BF16)
        ucb = sb("ucb", [128, 3, 2 + TWMAX], BF16)
        vb = sb("vb", [128, 2, 16 + TWMAX], BF16)
        ubs = sb("ubs", [128, 3, 31, NS], BF16)
        zsb = sb("zsb", [128, 2, 16, NS], BF16)
        scb = sb("scb", [128, 3, 3, NS], BF16)
        ubt = sb("ubt", [128, L, 3, 46], F32)
        vt = sb("vt", [128, L, 2, 31], F32)
        uct = sb("uct", [128, L, 3, 18], F32)
        sq = sb("sq", [128, 2, TWMAX], BF16)
        sg = sb("sg", [128, TWMAX], F32)
        cb = sb("cb", [128, 3, TWMAX], F32)
        cbb = sb("cbb", [128, 2, TWMAX], BF16)
        csq = sb("csq", [128, 2, TWMAX], BF16)
        msq = sb("msq", [128, TWMAX], F32)
        sa = sb("sa", [128, TWMAX], F32)
        mneg = sb("mneg", [128, TWMAX], F32)
        rsn = sb("rsn", [128, TWMAX], F32)
        xc = sb("xc", [128, TWMAX], F32)
        szc = sb("szc", [128, TWMAX], F32)
        tt = sb("tt", [128, TWMAX], F32)
        gt = sb("gt", [128, 2, TWMAX], F32)
        banks = [es.enter_context(nc.psum_tensor(f"ps{i}", [128, 512], F32)) for i in range(8)]

        TR = Tracker()

        def sem(name):
            s = es.enter_context(nc.semaphore(name))
            TR.add_sem(name, s)
            return s

        for nm in ["pe_s", "act_s", "dve_s", "pool_s", "prm_q", "out_q", "st_q", "wple_q", "setup_q"]:
            sem(nm)
        for n in range(NT):
            sem(f"x{n}")
        for i_ in range(3):
            sem(f"x0p{i_}")
        for g in range(len(WIN_GROUPS)):
            sem(f"win{g}")
        for s in range(RING):
            sem(f"ring{s}")
        for s in range(2):
            sem(f"pe{s}")

        PE = Eng("pe", nc.tensor, "pe_s", is_pe=True)
        ACT = Eng("act", nc.scalar, "act_s")
        DVE = Eng("dve", nc.vector, "dve_s")
        POOL = Eng("pool", nc.gpsimd, "pool_s")
        SP = Eng("sp", nc.sync, None)

        B = {}

        def buf(name):
            if name not in B:
                B[name] = Buf(name)
            return B[name]

        bank_buf = [Buf(f"bank{i}") for i in range(8)]
        bank_free_order = [0] * 8
        bank_busy = [False] * 8
        order = [0]

        def alloc_bank():
            best = None
            for i in range(8):
                if not bank_busy[i] and (best is None or bank_free_order[i] < bank_free_order[best]):
                    best = i
            assert best is not None, "out of PSUM banks"
            bank_busy[best] = True
            return best

        def release(i):
            order[0] += 1
            bank_busy[i] = False
            bank_free_order[i] = order[0]

        MMC = {"A": 0, "B": 0}
        CUR = ["A"]

        def mm(bi, c0, c1, lhsT, rhs, start, stop, reads, final):
            MMC[CUR[0]] += 1
            TR.op(PE, lambda t: t.matmul(banks[bi][:, c0:c1], lhsT=lhsT, rhs=rhs, start=start, stop=stop),
                  reads=reads, writes=[bank_buf[bi]], final=final)

        def pcol(i):
            return prm[:, i:i + 1]

        blocks = []
        for l in range(L):
            for n in range(NT):
                if l == L - 1 and n == NT - 1:
                    continue
                for m in range(8):
                    blocks.append((l, "o", m))
                for m in range(8):
                    blocks.append((l, "g", m))
        ring_issued = [0]

        def ring_issue():
            i = ring_issued[0]
            if i >= len(blocks):
                return
            l, kind, m = blocks[i]
            src = (w_out if kind == "o" else w_gate)[l].rearrange("(k p) n -> p k n", p=128)[:, :, m * 128:(m + 1) * 128]
            s = i % RING
            TR.dma(POOL, ring[:, s], src, f"ring{s}", writes=[buf(f"ring{s}")])
            ring_issued[0] += 1

        ring_used = [0]

        def ring_next():
            i = ring_used[0]
            assert i < ring_issued[0]
            ring_used[0] += 1
            return i % RING

        def plan():
            prm_grp = []
            TR.dma(SP, prm[:], prm_d, "prm_q", writes=[buf("prm")], group=prm_grp)
            TR.dma(SP, stage[:], w_pool.rearrange("l (j hh) c d -> (hh c) l j d", hh=2), "prm_q",
                   writes=[buf("stage")], group=prm_grp)
            TR.close_group(prm_grp, "prm_q")
            def load_x(n, gate=()):
                TR.dma(SP, x[:, :, TOFF[n]:TOFF[n] + TWS[n]],
                       xT.rearrange("(c p) t -> p c t", p=128)[:, :, TOFF[n]:TOFF[n] + TWS[n]], f"x{n}",
                       reads=list(gate), writes=[buf(f"x{c}_{n}") for c in range(8)])

            xv0 = xT.rearrange("(c p) t -> p c t", p=128)
            for i_ in range(4):
                TR.dma(SP, x[:, 2 * i_:2 * i_ + 2, 0:TWS[0]], xv0[:, 2 * i_:2 * i_ + 2, 0:TWS[0]],
                       "x0" if i_ == 0 else f"x0p{i_ - 1}", writes=[buf(f"x{c}_0") for c in (2 * i_, 2 * i_ + 1)])
            out_grp = []
            for l in range(L):
                TR.dma(SP, o_pool_old[l], st_pool_n[l, :, 1:15, :], "out_q", group=out_grp)
                TR.dma(SP, o_conv_old[l], st_conv_n[l, :, 1:30, :], "out_q", group=out_grp)
                TR.dma(SP, o_sconv_old[l], st_sconv_n[l, :, 1:2, :], "out_q", group=out_grp)

            def load_win(l):
                for g, (c0, c1) in enumerate(WIN_GROUPS):
                    TR.dma(POOL, win[:, :, c0:c1], w_in[l].rearrange("(k p) n -> p k n", p=128)[:, :, c0:c1],
                           f"win{g}", writes=[buf(f"win{g}")])

            def load_win_group(l, g, gate=()):
                c0, c1 = WIN_GROUPS[g]
                TR.dma(POOL, win[:, :, c0:c1], w_in[l].rearrange("(k p) n -> p k n", p=128)[:, :, c0:c1],
                       f"win{g}", reads=list(gate), writes=[buf(f"win{g}")])

            def load_last_blocks(group):
                grp = []
                first = True
                for i_, (g_, c_) in LAST_BLK.items():
                    if g_ != group:
                        continue
                    kind, m_ = ("o", i_) if i_ < 8 else ("g", i_ - 8)
                    src = (w_out if kind == "o" else w_gate)[L - 1].rearrange("(k p) n -> p k n", p=128)[:, :, m_ * 128:(m_ + 1) * 128]
                    wr = [buf(f"lb{i_}")] + ([buf(f"win{g_}")] if first else [])
                    TR.dma(POOL, win[:, :, c_:c_ + 128], src, f"win{g_}", writes=wr, group=grp)
                    first = False
                TR.close_group(grp, f"win{group}")

            def load_wple(l):
                TR.dma(POOL, wple[:], w_ple[l].rearrange("(k p) n -> p k n", p=128), "wple_q", writes=[buf("wple")])

            def load_states(l):
                grp = []
                TR.dma(POOL, ubs[:, :, 0:30, :], st_conv_f[l].rearrange("(j p) r s -> p j r s", p=128), "st_q",
                       writes=[buf("ubs")], group=grp)
                TR.dma(POOL, zsb[:, :, 0:15, :], st_pool_f[l].rearrange("(j p) r s -> p j r s", p=128), "st_q",
                       writes=[buf("zsb")], group=grp)
                TR.dma(POOL, scb[:, :, 0:2, :], st_sconv_f[l].rearrange("(j p) r s -> p j r s", p=128), "st_q",
                       writes=[buf("scb")], group=grp)
                TR.close_group(grp, "st_q")

            def build_d31(l, j, eng=None):
                eng = eng or POOL
                for k in range(31):
                    idx = j * 31 + k
                    sc = pcol(P_CW + l * 93 + idx)
                    rd, wr = [buf("ident"), buf("prm")], [buf(f"d31_{j}")]
                    if eng is POOL:
                        TR.op(POOL, lambda g, idx=idx, sc=sc: g.tensor_scalar(
                            out=d31[:, idx, :], in0=ident[:], scalar1=sc, scalar2=0.0,
                            op0=ALU.mult, op1=ALU.add), reads=rd, writes=wr, skip_self=True)
                    elif eng is DVE:
                        TR.op(DVE, lambda v, idx=idx, sc=sc: v.tensor_scalar(
                            out=d31[:, idx, :], in0=ident[:], scalar1=sc, scalar2=None, op0=ALU.mult),
                            reads=rd, writes=wr, skip_self=True)
                    else:
                        TR.op(ACT, lambda a, idx=idx, sc=sc: a.activation(
                            out=d31[:, idx, :], in_=ident[:], func=AF.Copy, scale=sc), reads=rd, writes=wr, skip_self=True)

            def build_d3(l):
                for j in range(3):
                    for k in range(3):
                        idx = j * 3 + k
                        TR.op(POOL, lambda g, idx=idx, l=l: g.tensor_scalar(
                            out=d3[:, idx, :], in0=ident[:], scalar1=pcol(P_SW + l * 9 + idx), scalar2=0.0,
                            op0=ALU.mult, op1=ALU.add), reads=[buf("ident"), buf("prm")], writes=[buf(f"d3_{j}")], skip_self=True)

            def build_pm(l):
                for j in range(2):
                    w_lo, w_hi = ((2, 4), (8, 16))[j]
                    specs = [(0, 64, j * 3 + 0, 1.0 / w_lo), (64, 128, j * 3 + 0, 1.0 / w_hi),
                             (64, 128, j * 3 + 1, 1.0 / w_hi), (0, 64, j * 3 + 2, -1.0), (64, 128, j * 3 + 2, -1.0)]
                    for (p0, p1, mi, sc) in specs:
                        TR.op(POOL, lambda g, p0=p0, p1=p1, mi=mi, sc=sc, l=l, j=j: g.tensor_scalar(
                            out=pm[p0:p1, mi, p0:p1], in0=stage[p0:p1, l, j, :], scalar1=float(sc), scalar2=0.0,
                            op0=ALU.mult, op1=ALU.add), reads=[buf("stage")], writes=[buf(f"pm_{j}")])

            def build_diags(l):
                build_d3(l)

            TR.op(DVE, lambda v: v.memset(ones[:], 1.0), writes=[buf("ones")])
            TR.op(DVE, lambda v: v.memset(epsT[:], EPS), writes=[buf("eps")])
            TR.op(ACT, lambda a: a.activation(out=t15[:, 0, 0:1], in_=epsT[:], func=AF.Sqrt), reads=[buf("eps")], writes=[buf("t15_0")])
            load_win_group(0, 0, gate=[buf("x1_0")])
            load_win_group(0, 1)
            TR.op(POOL, lambda g: g.affine_select(out=ident[:], in_=ones[:], pattern=[[1, 128]], compare_op=ALU.is_equal,
                                                  fill=0.0, base=0, channel_multiplier=-1),
                  reads=[buf("ones")], writes=[buf("ident")])
            for g_ in range(2, 5):
                load_win_group(0, g_)
            load_x(1, gate=[buf("win3")])
            build_d31(0, 1, POOL)
            build_d31(0, 2, POOL)
            TR.op(POOL, lambda g: g.memset(pm[:], 0.0), writes=[buf("pm_0"), buf("pm_1")])
            build_pm(0)
            for g_ in range(5, len(WIN_GROUPS)):
                load_win_group(0, g_)
            load_wple(0)
            for _i in range(RING):
                ring_issue()
            load_states(0)
            TR.op(POOL, lambda g: g.memset(rtab[:], 1.0), writes=[buf("rtab")])
            for j in range(2):
                for hh in range(2):
                    w = ((2, 4), (8, 16))[j][hh]
                    for t in range(w - 1):
                        TR.op(POOL, lambda g, j=j, hh=hh, t=t, w=w: g.memset(rtab[hh * 64:(hh + 1) * 64, j, t:t + 1],
                                                                          float(w) / (t + 1)), writes=[buf("rtab")])
            build_diags(0)

            def geom(n):
                Wn = TWS[n]
                last = (n == NT - 1)
                PW = Wn - NS if last else Wn
                return Wn, TOFF[n], PW, last

            def win_group_of(col):
                for g, (c0, c1) in enumerate(WIN_GROUPS):
                    if c0 <= col < c1:
                        return g
                raise AssertionError

            def inproj(col0, Wn, hb):
                bi = alloc_bank()
                g = win_group_of(col0)
                for k in range(8):
                    mm(bi, 0, Wn, win[:, k, col0:col0 + 128], h[:, hb, k, :Wn], k == 0, k == 7,
                       [buf(f"win{g}"), buf(f"h{hb}")], k == 7)
                return bi

            def recip(bi, Wn):
                bb = bank_buf[bi]
                if RECIP_MODE == 0:
                    TR.op(DVE, lambda v: v.reciprocal(out=banks[bi][:, :Wn], in_=banks[bi][:, :Wn]), reads=[bb], writes=[bb])
                else:
                    TR.op(DVE, lambda v: v.reciprocal_approx_accurate(out=banks[bi][:, :Wn], in_=banks[bi][:, :Wn],
                                                                      scratch=rscr[:, :Wn]),
                          reads=[bb], writes=[bb, buf("rscr")])

            def norm_head(n, scr, scr_bufs):
                Wn, On, PW, last = geom(n)
                cs = slice(On, On + Wn)
                for c in range(8):
                    TR.op(ACT, lambda a, c=c: a.activation(out=scr(c)[:, :Wn], in_=x[:, c, cs], func=AF.Square),
                          reads=[buf(f"x{c}_{n}")], writes=[scr_bufs[c]])
                bS = alloc_bank()
                for c in range(8):
                    mm(bS, 0, Wn, ones[:], scr(c)[:, :Wn], c == 0, c == 7, [buf("ones"), scr_bufs[c]], c == 7)
                return bS

            def norm_sqrt(bS, n):
                Wn = TWS[n]
                bb = bank_buf[bS]
                TR.op(ACT, lambda a: a.activation(out=rsn[:, :Wn], in_=banks[bS][:, :Wn], func=AF.Sqrt,
                                                  bias=epsT[:], scale=1.0 / D), reads=[bb, buf("eps")], writes=[buf("rsn")])
                release(bS)

            def norm_recip(n):
                Wn = TWS[n]
                TR.op(DVE, lambda v: v.reciprocal(out=rsn[:, :Wn], in_=rsn[:, :Wn]), reads=[buf("rsn")], writes=[buf("rsn")])

            def norm_apply(l, n, bS, hb, final_norm):
                Wn, On, PW, last = geom(n)
                cs = slice(On, On + Wn)
                for c in range(8):
                    if final_norm:
                        TR.op(DVE, lambda v, c=c: v.scalar_tensor_tensor(
                            out=x[:, c, cs], in0=x[:, c, cs], scalar=pcol(P_FG + c), in1=rsn[:, :Wn],
                            op0=ALU.mult, op1=ALU.mult), reads=[buf(f"x{c}_{n}"), buf("rsn"), buf("prm")], writes=[buf(f"x{c}_{n}")])
                    else:
                        TR.op(DVE, lambda v, c=c: v.scalar_tensor_tensor(
                            out=h[:, hb, c, :Wn], in0=x[:, c, cs], scalar=pcol(P_NG + l * 8 + c), in1=rsn[:, :Wn],
                            op0=ALU.mult, op1=ALU.mult), reads=[buf(f"x{c}_{n}"), buf("rsn"), buf("prm")], writes=[buf(f"h{hb}")])

            def rstd_in_psum(bS, Wn):
                bb = bank_buf[bS]
                TR.op(ACT, lambda a: a.activation(out=banks[bS][:, :Wn], in_=banks[bS][:, :Wn], func=AF.Sqrt,
                                                  bias=epsT[:], scale=1.0 / D), reads=[bb, buf("eps")], writes=[bb])
                TR.op(DVE, lambda v: v.reciprocal(out=banks[bS][:, :Wn], in_=banks[bS][:, :Wn]), reads=[bb], writes=[bb])

            def norm_full(l, n, hb, final_norm=False):
                Wn, On, PW, last = geom(n)
                cs = slice(On, On + Wn)
                bS = norm_head(n, lambda c: h[:, hb, c, :], [buf(f"h{hb}")] * 8)
                bw = alloc_bank()
                for i_ in range(WARM_MM):
                    mm(bw, 0, 128, ones[:], ident[:], True, True, [buf("ones"), buf("ident")], i_ == WARM_MM - 1)
                release(bw)
                rstd_in_psum(bS, Wn)
                for c in range(8):
                    TR.op(DVE, lambda v, c=c: v.scalar_tensor_tensor(
                        out=h[:, hb, c, :Wn], in0=x[:, c, cs], scalar=pcol(P_NG + l * 8 + c), in1=banks[bS][:, :Wn],
                        op0=ALU.mult, op1=ALU.mult), reads=[buf(f"x{c}_{n}"), bank_buf[bS], buf("prm")], writes=[buf(f"h{hb}")])
                release(bS)

            def A_tile(l, n):
                g = l * NT + n
                hb = g % 2
                Wn, On, PW, last = geom(n)
                Wp = TWMAX
                boundary = last and (l + 1 < L)
                slot = g % 2
                if l == 0 and n + 2 < NT:
                    load_x(n + 2, gate=[buf(f"h{hb}")])
                TR.dma(POOL, pe[:, slot, :, :Wn], peT[l].rearrange("(k p) t -> p k t", p=128)[:, :, On:On + Wn],
                       f"pe{slot}", writes=[buf(f"pe{slot}")])
                for j in range(3):
                    bg = inproj(C_GB + 128 * j, Wn, hb)
                    TR.op(ACT, lambda a, bg=bg: a.activation(out=sg[:, :Wn], in_=banks[bg][:, :Wn], func=AF.Tanh, scale=0.5),
                          reads=[bank_buf[bg]], writes=[buf("sg")])
                    release(bg)
                    TR.op(DVE, lambda v: v.tensor_scalar(out=sg[:, :Wn], in0=sg[:, :Wn], scalar1=0.5, scalar2=0.5,
                                                         op0=ALU.mult, op1=ALU.add), reads=[buf("sg")], writes=[buf("sg")])
                    yield
                    ba = inproj(C_AB + 128 * j, Wn, hb)
                    if n == 0:
                        TR.op(DVE, lambda v, j=j: v.memset(ub[:, j, 0:30], 0.0), writes=[buf(f"ub{j}")])
                    else:
                        TR.op(DVE, lambda v, j=j: v.tensor_copy(out=ub[:, j, 0:30], in_=ub[:, j, Wp:Wp + 30]),
                              reads=[buf(f"ub{j}")], writes=[buf(f"ub{j}")])
                    TR.op(DVE, lambda v, j=j, ba=ba: v.tensor_tensor(out=ub[:, j, 30:30 + Wn], in0=banks[ba][:, :Wn],
                                                                     in1=sg[:, :Wn], op=ALU.mult),
                          reads=[bank_buf[ba], buf("sg")], writes=[buf(f"ub{j}")])
                    if last:
                        TR.op(DVE, lambda v, j=j, ba=ba: v.tensor_tensor(out=ubt[:, l, j, :], in0=banks[ba][:, PW - 30:Wn],
                                                                         in1=sg[:, PW - 30:Wn], op=ALU.mult),
                              reads=[bank_buf[ba], buf("sg")], writes=[buf(f"ubt{l}")])
                        TR.op(DVE, lambda v, j=j: v.tensor_copy(out=ubs[:, j, 30, :], in_=ub[:, j, 30 + PW:30 + Wn]),
                              reads=[buf(f"ub{j}")], writes=[buf("ubs")])
                    release(ba)
                    yield
                if boundary:
                    load_win_group(l + 1, 0)
                    load_win_group(l + 1, 1)
                if l == L - 1 and last:
                    load_last_blocks(0)
                    load_last_blocks(1)
                if l >= 1 and n == 0:
                    build_pm(l)
                    load_win_group(l, 5)
                    load_win_group(l, 6)
                    load_win_group(l, 7)
                def c_part1(j):
                    bx = inproj(C_XC + 128 * j, Wn, hb)
                    TR.op(ACT, lambda a, bx=bx: a.activation(out=xc[:, :Wn], in_=banks[bx][:, :Wn], func=AF.Copy),
                          reads=[bank_buf[bx]], writes=[buf("xc")])
                    release(bx)
                    yield 0
                    bc = inproj(C_CC + 128 * j, Wn, hb)
                    if n == 0:
                        TR.op(DVE, lambda v, j=j: v.memset(ucb[:, j, 0:2], 0.0), writes=[buf(f"ucb{j}")])
                    else:
                        TR.op(DVE, lambda v, j=j: v.tensor_copy(out=ucb[:, j, 0:2], in_=ucb[:, j, Wp:Wp + 2]),
                              reads=[buf(f"ucb{j}")], writes=[buf(f"ucb{j}")])
                    TR.op(DVE, lambda v, j=j, bc=bc: v.tensor_tensor(out=ucb[:, j, 2:2 + Wn], in0=banks[bc][:, :Wn],
                                                                     in1=xc[:, :Wn], op=ALU.mult),
                          reads=[bank_buf[bc], buf("xc")], writes=[buf(f"ucb{j}")])
                    if last:
                        TR.op(DVE, lambda v, j=j, bc=bc: v.tensor_tensor(out=uct[:, l, j, :], in0=banks[bc][:, PW - 2:Wn],
                                                                         in1=xc[:, PW - 2:Wn], op=ALU.mult),
                              reads=[bank_buf[bc], buf("xc")], writes=[buf(f"uct{l}")])
                        TR.op(DVE, lambda v, j=j: v.tensor_copy(out=scb[:, j, 2, :], in_=ucb[:, j, 2 + PW:2 + Wn]),
                              reads=[buf(f"ucb{j}")], writes=[buf("scb")])
                    release(bc)
                    yield 1

                def va_chunk(j):
                    bv = inproj(C_VA + 128 * j, Wn, hb)
                    if n == 0:
                        TR.op(DVE, lambda v, j=j: v.memset(vb[:, j, 0:16], 0.0), writes=[buf(f"vb{j}")])
                    else:
                        TR.op(DVE, lambda v, j=j: v.tensor_copy(out=vb[:, j, 1:16], in_=vb[:, j, Wp + 1:Wp + 16]),
                              reads=[buf(f"vb{j}")], writes=[buf(f"vb{j}")])
                    TR.op(ACT, lambda a, j=j, bv=bv: a.activation(out=vb[:, j, 16:16 + Wn], in_=banks[bv][:, :Wn], func=AF.Copy),
                          reads=[bank_buf[bv]], writes=[buf(f"vb{j}")])
                    if last:
                        TR.op(ACT, lambda a, j=j, bv=bv: a.activation(out=vt[:, l, j, :], in_=banks[bv][:, PW - 15:Wn], func=AF.Copy),
                              reads=[bank_buf[bv]], writes=[buf(f"vt{l}")])
                        TR.op(ACT, lambda a, j=j, bv=bv: a.activation(out=zsb[:, j, 15, :], in_=banks[bv][:, PW:Wn], func=AF.Copy),
                              reads=[bank_buf[bv]], writes=[buf("zsb")])
                    release(bv)

                bsx = {}

                def stats(j):
                    if j == 0:
                        bsx[1] = alloc_bank()
                        bsx[2] = alloc_bank()
                    mm(bsx[1], 0, Wn, ones[:], cbb[:, j % 2, :Wn], j == 0, j == 2, [buf("ones"), buf(f"cbb{j % 2}")], True)
                    mm(bsx[2], 0, Wn, ones[:], csq[:, j % 2, :Wn], j == 0, j == 2, [buf("ones"), buf(f"csq{j % 2}")], True)

                for j in range(3):
                    bcv = alloc_bank()
                    for k in range(31):
                        mm(bcv, 0, PW, d31[:, j * 31 + k, :], ub[:, j, k:k + PW], k == 0, k == 30,
                           [buf(f"d31_{j}"), buf(f"ub{j}")], (k == 30 and not last))
                    if last:
                        for k in range(31):
                            mm(bcv, PW, Wn, d31[:, j * 31 + k, :], ubs[:, j, k, :], k == 0, k == 30,
                               [buf(f"d31_{j}"), buf("ubs")], k == 30)
                    bias = pcol(P_CB + l * 3 + j)
                    TR.op(ACT, lambda a, j=j, bcv=bcv, bias=bias: a.activation(
                        out=cb[:, j, :Wn], in_=banks[bcv][:, :Wn], func=AF.Identity, bias=bias),
                        reads=[bank_buf[bcv], buf("prm")], writes=[buf(f"cb{j}")])
                    TR.op(ACT, lambda a, j=j, bcv=bcv, bias=bias: a.activation(
                        out=cbb[:, j % 2, :Wn], in_=banks[bcv][:, :Wn], func=AF.Identity, bias=bias),
                        reads=[bank_buf[bcv], buf("prm")], writes=[buf(f"cbb{j % 2}")])
                    TR.op(ACT, lambda a, j=j, bcv=bcv, bias=bias: a.activation(
                        out=csq[:, j % 2, :Wn], in_=banks[bcv][:, :Wn], func=AF.Square, bias=bias),
                        reads=[bank_buf[bcv], buf("prm")], writes=[buf(f"csq{j % 2}")])
                    release(bcv)
                    if boundary:
                        build_d31(l + 1, j, DVE)
                    if j >= 1:
                        stats(j - 1)
                    yield

                nxt = (l, n + 1) if n + 1 < NT else ((l + 1, 0) if l + 1 < L else None)
                st = {}
                chain = []

                hn = (g + 1) % 2
                has_n = nxt is not None

                def c_mneg():
                    bs1 = bsx[1]
                    TR.op(DVE, lambda v: v.tensor_scalar(out=mneg[:, :Wn], in0=banks[bs1][:, :Wn], scalar1=-1.0 / W_B,
                                                         scalar2=None, op0=ALU.mult), reads=[bank_buf[bs1]], writes=[buf("mneg")])
                    release(bs1)

                def c_msq():
                    TR.op(ACT, lambda a: a.activation(out=msq[:, :Wn], in_=mneg[:, :Wn], func=AF.Square),
                          reads=[buf("mneg")], writes=[buf("msq")])

                def c_var():
                    bs2 = bsx[2]
                    TR.op(DVE, lambda v: v.scalar_tensor_tensor(out=msq[:, :Wn], in0=banks[bs2][:, :Wn], scalar=1.0 / W_B,
                                                                in1=msq[:, :Wn], op0=ALU.mult, op1=ALU.subtract),
                          reads=[bank_buf[bs2], buf("msq")], writes=[buf("msq")])
                    release(bs2)

                def c_nsq(ca=0, cz=8):
                    if has_n:
                        n2 = nxt[1]
                        W2, O2 = TWS[n2], TOFF[n2]
                        for c in range(ca, cz):
                            TR.op(ACT, lambda a, c=c: a.activation(out=h[:, hn, c, :W2], in_=x[:, c, O2:O2 + W2], func=AF.Square),
                                  reads=[buf(f"x{c}_{n2}")], writes=[buf(f"h{hn}")])

                def c_nmm():
                    if has_n:
                        n2 = nxt[1]
                        W2 = TWS[n2]
                        bS = alloc_bank()
                        for c in range(8):
                            mm(bS, 0, W2, ones[:], h[:, hn, c, :W2], c == 0, c == 7, [buf("ones"), buf(f"h{hn}")], c == 7)
                        st["bS"] = bS

                def c_sqrts():
                    TR.op(ACT, lambda a: a.activation(out=msq[:, :Wn], in_=msq[:, :Wn], func=AF.Relu),
                          reads=[buf("msq")], writes=[buf("msq")])
                    TR.op(ACT, lambda a: a.activation(out=msq[:, :Wn], in_=msq[:, :Wn], func=AF.Sqrt, bias=epsT[:],
                                                      scale=1.0), reads=[buf("msq"), buf("eps")], writes=[buf("msq")])
                    if has_n:
                        norm_sqrt(st["bS"], nxt[1])

                def c_rec_ln():
                    TR.op(DVE, lambda v: v.reciprocal(out=msq[:, :Wn], in_=msq[:, :Wn]), reads=[buf("msq")], writes=[buf("msq")])

                def c_rec_n():
                    if has_n:
                        norm_recip(nxt[1])

                def c_ln_dve(j):
                    cj = buf(f"cb{j}")
                    TR.op(DVE, lambda v: v.tensor_tensor(out=cb[:, j, :Wn], in0=cb[:, j, :Wn], in1=mneg[:, :Wn],
                                                         op=ALU.add), reads=[cj, buf("mneg")], writes=[cj])
                    TR.op(DVE, lambda v: v.tensor_tensor(out=cb[:, j, :Wn], in0=cb[:, j, :Wn], in1=msq[:, :Wn],
                                                         op=ALU.mult), reads=[cj, buf("msq")], writes=[cj])

                def c_ln_act(j):
                    cj = buf(f"cb{j}")
                    TR.op(ACT, lambda a: a.activation(out=cb[:, j, :Wn], in_=cb[:, j, :Wn], func=AF.Silu,
                                                      bias=pcol(P_LB + l * 3 + j), scale=pcol(P_LG + l * 3 + j)),
                          reads=[cj, buf("prm")], writes=[cj])

                def c_napply(c0, c1):
                    if has_n:
                        n2 = nxt[1]
                        W2, O2 = TWS[n2], TOFF[n2]
                        for c in range(c0, c1):
                            TR.op(DVE, lambda v, c=c: v.scalar_tensor_tensor(
                                out=h[:, hn, c, :W2], in0=x[:, c, O2:O2 + W2], scalar=pcol(P_NG + nxt[0] * 8 + c), in1=rsn[:, :W2],
                                op0=ALU.mult, op1=ALU.mult), reads=[buf(f"x{c}_{n2}"), buf("rsn"), buf("prm")], writes=[buf(f"h{hn}")])

                chain += [lambda: (c_mneg(), c_nsq(0, 3)), lambda: (c_msq(), c_nsq(3, 6)), lambda: (c_var(), c_nsq(6, 8)),
                          lambda: None, c_nmm, lambda: None, c_sqrts, c_rec_ln, c_rec_n,
                          lambda: c_ln_dve(0), lambda: c_ln_dve(1), lambda: (c_ln_dve(2), c_ln_act(0)),
                          lambda: (c_napply(0, 2), c_ln_act(1)), lambda: (c_napply(2, 4), c_ln_act(2)),
                          lambda: c_napply(4, 6), lambda: c_napply(6, 8)]

                def drip(k=1):
                    for _ in range(k):
                        if chain:
                            chain.pop(0)()

                va_chunk(0)
                stats(2)
                yield
                va_chunk(1)
                drip()
                yield
                for j in range(3):
                    for _half in c_part1(j):
                        drip()
                        yield
                if boundary:
                    load_win_group(l + 1, 3)
                    load_win_group(l + 1, 4)
                if l >= 1 and n == 0:
                    build_d3(l)
                if l >= 1 and n == 1:
                    load_states(l)
                if l == L - 1 and last:
                    load_last_blocks(3)
                    load_last_blocks(4)
                bfix = None
                for j in range(2):
                    bz = inproj(C_ZA + 128 * j, Wn, hb)
                    TR.op(ACT, lambda a, j=j, bz=bz: a.activation(out=sa[:, :Wn], in_=banks[bz][:, :Wn], func=AF.Silu),
                          reads=[bank_buf[bz]], writes=[buf("sa")])
                    release(bz)
                    drip()
                    yield
                    w_lo, w_hi = ((2, 4), (8, 16))[j]
                    bm = alloc_bank()
                    rd = [buf(f"pm_{j}"), buf(f"vb{j}")]
                    for d in range(w_hi):
                        mi = j * 3 + (0 if d < w_lo else 1)
                        mm(bm, 0, PW, pm[:, mi, :], vb[:, j, 16 - d:16 - d + PW], d == 0, False, rd, False)
                    mm(bm, 0, PW, pm[:, j * 3 + 2, :], vb[:, j, 16:16 + PW], False, True, rd, not last)
                    if last:
                        rd2 = [buf(f"pm_{j}"), buf("zsb")]
                        for d in range(w_hi):
                            mi = j * 3 + (0 if d < w_lo else 1)
                            mm(bm, PW, Wn, pm[:, mi, :], zsb[:, j, 15 - d, :], d == 0, False, rd2, False)
                        mm(bm, PW, Wn, pm[:, j * 3 + 2, :], zsb[:, j, 15, :], False, True, rd2, True)
                    if n == 0:
                        if bfix is None:
                            bfix = alloc_bank()
                        c0 = j * 64
                        for d in range(w_hi):
                            mi = j * 3 + (0 if d < w_lo else 1)
                            mm(bfix, c0, c0 + 15, pm[:, mi, :], vb[:, j, 16 - d:31 - d], d == 0, d == w_hi - 1, rd, False)
                        mm(bfix, c0 + 16, c0 + 31, pm[:, j * 3 + 2, :], vb[:, j, 16:31], True, True, rd, True)
                    TR.op(DVE, lambda v, j=j, bm=bm: v.scalar_tensor_tensor(
                        out=ycat[:, j, :Wn], in0=banks[bm][:, :Wn], scalar=pcol(P_PS + l * 2 + j), in1=sa[:, :Wn],
                        op0=ALU.mult, op1=ALU.mult), reads=[bank_buf[bm], buf("sa"), buf("prm")], writes=[buf(f"ycat{j}")])
                    release(bm)
                    if n == 0:
                        c0 = j * 64
                        fb = bank_buf[bfix]
                        TR.op(DVE, lambda v, j=j, c0=c0: v.tensor_tensor(out=t15[:, j, :], in0=banks[bfix][:, c0:c0 + 15],
                                                                         in1=rtab[:, j, :], op=ALU.mult),
                              reads=[fb, buf("rtab")], writes=[buf(f"t15_{j}")])
                        TR.op(DVE, lambda v, j=j, c0=c0: v.tensor_tensor(out=t15[:, j, :], in0=t15[:, j, :],
                                                                         in1=banks[bfix][:, c0 + 16:c0 + 31], op=ALU.add),
                              reads=[fb, buf(f"t15_{j}")], writes=[buf(f"t15_{j}")])
                        TR.op(DVE, lambda v, j=j: v.scalar_tensor_tensor(
                            out=ycat[:, j, 0:15], in0=t15[:, j, :], scalar=pcol(P_PS + l * 2 + j), in1=sa[:, 0:15],
                            op0=ALU.mult, op1=ALU.mult), reads=[buf(f"t15_{j}"), buf("sa"), buf("prm")],
                            writes=[buf(f"ycat{j}")])
                    drip()
                    yield
                if n == 0:
                    release(bfix)
                if boundary:
                    load_win_group(l + 1, 2)
                if l == L - 1 and last:
                    load_last_blocks(2)
                for j in range(3):
                    bzc = inproj(C_ZC + 128 * j, Wn, hb)
                    TR.op(ACT, lambda a, bzc=bzc: a.activation(out=szc[:, :Wn], in_=banks[bzc][:, :Wn], func=AF.Silu),
                          reads=[bank_buf[bzc]], writes=[buf("szc")])
                    release(bzc)
                    drip()
                    yield
                    bbc = inproj(C_BC + 128 * j, Wn, hb)
                    TR.op(DVE, lambda v, bbc=bbc: v.tensor_tensor(out=tt[:, :Wn], in0=banks[bbc][:, :Wn], in1=szc[:, :Wn],
                                                                  op=ALU.mult), reads=[bank_buf[bbc], buf("szc")], writes=[buf("tt")])
                    release(bbc)
                    bsc = alloc_bank()
                    for k in range(3):
                        mm(bsc, 0, PW, d3[:, j * 3 + k, :], ucb[:, j, k:k + PW], k == 0, k == 2,
                           [buf(f"d3_{j}"), buf(f"ucb{j}")], (k == 2 and not last))
                    if last:
                        for k in range(3):
                            mm(bsc, PW, Wn, d3[:, j * 3 + k, :], scb[:, j, k, :], k == 0, k == 2,
                               [buf(f"d3_{j}"), buf("scb")], k == 2)
                    TR.op(DVE, lambda v, j=j, bsc=bsc: v.tensor_tensor(out=ycat[:, 5 + j, :Wn], in0=banks[bsc][:, :Wn],
                                                                       in1=tt[:, :Wn], op=ALU.mult),
                          reads=[bank_buf[bsc], buf("tt")], writes=[buf(f"ycat{5 + j}")])
                    release(bsc)
                    drip()
                    yield
                if boundary:
                    pass
                drip(len(chain))
                for j in range(3):
                    bz = inproj(C_ZB + 128 * j, Wn, hb)
                    zb_ = bank_buf[bz]
                    TR.op(ACT, lambda a, bz=bz: a.activation(out=banks[bz][:, :Wn], in_=banks[bz][:, :Wn], func=AF.Silu),
                          reads=[zb_], writes=[zb_])
                    TR.op(DVE, lambda v, j=j, bz=bz: v.tensor_tensor(out=ycat[:, 2 + j, :Wn], in0=cb[:, j, :Wn],
                                                                     in1=banks[bz][:, :Wn], op=ALU.mult),
                          reads=[zb_, buf(f"cb{j}")], writes=[buf(f"ycat{2 + j}")])
                    release(bz)
                    yield

            def B_tile(l, n):
                g = l * NT + n
                Wn, On, PW, last = geom(n)
                cs = slice(On, On + Wn)
                slot = g % 2
                ycs = [buf(f"ycat{k}") for k in range(8)]
                very_last = (l == L - 1 and n == NT - 1)
                for m in range(8):
                    bo = alloc_bank()
                    KORD = [0, 1, 5, 6, 7, 2, 3, 4]
                    if very_last:
                        _g, _c = LAST_BLK[m]
                        for ki, k in enumerate(KORD):
                            mm(bo, 0, Wn, win[:, k, _c:_c + 128], ycat[:, k, :Wn], ki == 0, ki == 7, [buf(f"lb{m}"), ycs[k]], ki == 7)
                    else:
                        rs = ring_next()
                        for ki, k in enumerate(KORD):
                            mm(bo, 0, Wn, ring[:, rs, k, :], ycat[:, k, :Wn], ki == 0, ki == 7, [buf(f"ring{rs}"), ycs[k]], ki == 7)
                        ring_issue()
                    xm = buf(f"x{m}_{n}")
                    TR.op(DVE, lambda v, m=m, bo=bo: v.tensor_tensor(out=x[:, m, cs], in0=banks[bo][:, :Wn], in1=x[:, m, cs],
                                                                     op=ALU.add), reads=[bank_buf[bo], xm], writes=[xm])
                    release(bo)
                    TR.op(ACT, lambda a, m=m: a.activation(out=xb[:, m, :Wn], in_=x[:, m, cs], func=AF.Copy),
                          reads=[xm], writes=[buf(f"xb{m}")])
                    yield
                for m in range(8):
                    bg = alloc_bank()
                    if very_last:
                        _g, _c = LAST_BLK[8 + m]
                        for k in range(8):
                            mm(bg, 0, Wn, win[:, k, _c:_c + 128], xb[:, k, :Wn], k == 0, k == 7, [buf(f"lb{8 + m}"), buf(f"xb{k}")], k == 7)
                    else:
                        rs = ring_next()
                        for k in range(8):
                            mm(bg, 0, Wn, ring[:, rs, k, :], xb[:, k, :Wn], k == 0, k == 7, [buf(f"ring{rs}"), buf(f"xb{k}")], k == 7)
                        ring_issue()
                    bp = alloc_bank()
                    for k in range(2):
                        mm(bp, 0, Wn, wple[:, k, m * 128:(m + 1) * 128], pe[:, slot, k, :Wn], k == 0, k == 1,
                           [buf("wple"), buf(f"pe{slot}")], k == 1)
                    TR.op(ACT, lambda a, m=m, bg=bg: a.activation(out=gt[:, m % 2, :Wn], in_=banks[bg][:, :Wn], func=AF.Tanh,
                                                                  scale=0.5), reads=[bank_buf[bg]], writes=[buf(f"gt{m % 2}")])
                    release(bg)
                    pb = bank_buf[bp]
                    TR.op(DVE, lambda v, m=m, bp=bp: v.scalar_tensor_tensor(
                        out=banks[bp][:, :Wn], in0=gt[:, m % 2, :Wn], scalar=1.0, in1=banks[bp][:, :Wn],
                        op0=ALU.add, op1=ALU.mult), reads=[pb, buf(f"gt{m % 2}")], writes=[pb])
                    xm = buf(f"x{m}_{n}")
                    TR.op(DVE, lambda v, m=m, bp=bp: v.scalar_tensor_tensor(
                        out=x[:, m, cs], in0=banks[bp][:, :Wn], scalar=0.5, in1=x[:, m, cs],
                        op0=ALU.mult, op1=ALU.add), reads=[pb, xm], writes=[xm])
                    release(bp)
                    if very_last:
                        for mq in ([m - 1] if m >= 1 else []) + ([7] if m == 7 else []):
                            TR.op(ACT, lambda a, mq=mq: a.activation(out=h[:, g % 2, mq, :Wn], in_=x[:, mq, cs], func=AF.Square),
                                  reads=[buf(f"x{mq}_{n}")], writes=[buf(f"h{g % 2}")])
                    yield
                if last and l + 1 < L:
                    load_wple(l + 1)
                if very_last:
                    bS = alloc_bank()
                    for c in range(8):
                        mm(bS, 0, Wn, ones[:], h[:, g % 2, c, :Wn], c == 0, c == 7, [buf("ones"), buf(f"h{g % 2}")], c == 7)
                    rstd_in_psum(bS, Wn)
                    yv = yT.rearrange("(c p) t -> p c t", p=128)
                    for c in range(8):
                        TR.op(DVE, lambda v, c=c: v.scalar_tensor_tensor(
                            out=x[:, c, cs], in0=x[:, c, cs], scalar=pcol(P_FG + c), in1=banks[bS][:, :Wn],
                            op0=ALU.mult, op1=ALU.mult), reads=[buf(f"x{c}_{n}"), bank_buf[bS], buf("prm")], writes=[buf(f"x{c}_{n}")])
                        TR.dma(SP, yv[:, c, On:On + Wn], x[:, c, On:On + Wn], "out_q", reads=[buf(f"x{c}_{n}")], group=out_grp)
                    release(bS)
                    yield
                elif l == L - 1:
                    for c in range(8):
                        TR.op(ACT, lambda a, c=c: a.activation(out=xb[:, c, :Wn], in_=x[:, c, cs], func=AF.Square),
                              reads=[buf(f"x{c}_{n}")], writes=[buf(f"xb{c}")])
                        if c == 3:
                            yield
                    yield
                    yield
                    bS = alloc_bank()
                    for c in range(8):
                        mm(bS, 0, Wn, ones[:], xb[:, c, :Wn], c == 0, c == 7, [buf("ones"), buf(f"xb{c}")], c == 7)
                    yield
                    yield
                    norm_sqrt(bS, n)
                    yield
                    norm_recip(n)
                    yield
                    norm_apply(l, n, bS, 0, True)
                    TR.dma(SP, yT.rearrange("(c p) t -> p c t", p=128)[:, :, On:On + Wn], x[:, :, On:On + Wn], "out_q",
                           reads=[buf(f"x{c}_{n}") for c in range(8)], group=out_grp)
                    yield

            def step(gen):
                try:
                    next(gen)
                    return False
                except StopIteration:
                    return True

            def merge(A, B, At, Bt):
                MMC["A"] = 0
                MMC["B"] = 0
                a_done = A is None
                b_done = B is None
                while not a_done:
                    CUR[0] = "A"
                    a_done = step(A)
                    if not b_done and MMC["B"] / Bt < B_RATIO * MMC["A"] / At:
                        CUR[0] = "B"
                        b_done = step(B)
                while not b_done:
                    CUR[0] = "B"
                    b_done = step(B)

            norm_full(0, 0, 0)
            if OPT_BUILD == 'mixed':
                build_d31(0, 0, DVE)
                build_d31(0, 1, ACT)
                build_d31(0, 2, DVE)
            elif OPT_BUILD == 'dve':
                build_d31(0, 0, DVE)
            else:
                for j_ in range(3):
                    build_d31(0, j_, POOL)
            prevB = None
            G = L * NT
            for g in range(G):
                l, n = divmod(g, NT)
                At = 338 + (124 if n == NT - 1 else 0) + (24 if n == 0 else 0)
                Bt = 144 + (8 if (prevB is not None and (g - 1) // NT == L - 1) else 0)
                merge(A_tile(l, n), prevB, At, Bt)
                prevB = B_tile(l, n)
                if n == NT - 1:
                    TR.dma(SP, o_conv_t[l].rearrange("(j p) c -> p j c", p=128), ubt[:, l], "out_q",
                           reads=[buf(f"ubt{l}")], group=out_grp)
                    TR.dma(SP, o_pool_t[l].rearrange("(j p) c -> p j c", p=128), vt[:, l], "out_q",
                           reads=[buf(f"vt{l}")], group=out_grp)
                    TR.dma(SP, o_sconv_t[l].rearrange("(j p) c -> p j c", p=128), uct[:, l], "out_q",
                           reads=[buf(f"uct{l}")], group=out_grp)
            merge(None, prevB, 1, 152)
            SP.prog.append(lambda hh, s_=TR.sems["out_q"], v_=TR.semval["out_q"]: hh.wait_ge(s_, v_))

        plan()
        check_deadlock([SP, POOL, ACT, DVE, PE])
        block = es.enter_context(nc.Block())

        def run(eng):
            def _f(hh):
                for f_ in eng.prog:
                    f_(hh)
            return _f

        block.sync(run(SP))
        block.gpsimd(run(POOL))
        block.scalar(run(ACT))
        block.vector(run(DVE))
        block.tensor(run(PE))
        print("prog sizes:", {e.name: len(e.prog) for e in (SP, POOL, ACT, DVE, PE)}, "sbuf left", nc.sbuf_bytes_remaining)
    return nc


_NC_CACHE = {}


def _get_nc():
    if "nc" not in _NC_CACHE:
        _NC_CACHE["nc"] = build_nc()
    return _NC_CACHE["nc"]


def kernel(x_prompt, x_sample, state_pool, state_conv, state_sconv, p_prompt, p_sample,
           norm_g, w_in, w_pool_mix, pool_scale, conv_b_w, conv_b_b, ln_b_g, ln_b_b,
           sconv_w, w_out, w_ple, w_ple_gate, final_norm_g):
    f = lambda a: np.ascontiguousarray(np.asarray(a, dtype=np.float32))
    x_prompt, x_sample, state_pool, state_conv, state_sconv = map(f, (x_prompt, x_sample, state_pool, state_conv, state_sconv))
    p_prompt, p_sample = f(p_prompt), f(p_sample)
    norm_g, w_in, w_pool_mix, pool_scale, conv_b_w, conv_b_b = map(f, (norm_g, w_in, w_pool_mix, pool_scale, conv_b_w, conv_b_b))
    ln_b_g, ln_b_b, sconv_w, w_out, w_ple, w_ple_gate, final_norm_g = map(
        f, (ln_b_g, ln_b_b, sconv_w, w_out, w_ple, w_ple_gate, final_norm_g))

    prm = np.zeros((128, NPRM), np.float32)
    prm[:, P_NG:P_NG + 16] = norm_g.reshape(L, 8, 128).transpose(2, 0, 1).reshape(128, 16)
    prm[:, P_FG:P_FG + 8] = final_norm_g.reshape(8, 128).T
    prm[:, P_PS:P_PS + 4] = pool_scale.reshape(L, 2, 128).transpose(2, 0, 1).reshape(128, 4)
    prm[:, P_CW:P_CW + 186] = conv_b_w.reshape(L, 31, 3, 128).transpose(3, 0, 2, 1).reshape(128, 186)
    prm[:, P_CB:P_CB + 6] = conv_b_b.reshape(L, 3, 128).transpose(2, 0, 1).reshape(128, 6)
    prm[:, P_LG:P_LG + 6] = ln_b_g.reshape(L, 3, 128).transpose(2, 0, 1).reshape(128, 6)
    prm[:, P_LB:P_LB + 6] = ln_b_b.reshape(L, 3, 128).transpose(2, 0, 1).reshape(128, 6)
    prm[:, P_SW:P_SW + 18] = sconv_w.reshape(L, 3, 3, 128).transpose(3, 0, 2, 1).reshape(128, 18)

    in_maps = []
    for c in range(NCORES):
        s0, s1 = c * NS, (c + 1) * NS
        xT = np.concatenate([x_prompt[c].T, x_sample[s0:s1, 0, :].T], axis=1)
        peT = np.stack([np.concatenate([p_prompt[l, c].T, p_sample[l, s0:s1, 0, :].T], axis=1) for l in range(L)])
        in_maps.append({
            "xT": f(xT), "peT": f(peT), "w_in": w_in, "w_out": w_out, "w_ple": w_ple, "w_gate": w_ple_gate,
            "w_pool": w_pool_mix, "prm": prm,
            "st_pool_f": f(state_pool[:, s0:s1].transpose(0, 3, 2, 1)),
            "st_conv_f": f(state_conv[:, s0:s1].transpose(0, 3, 2, 1)),
            "st_sconv_f": f(state_sconv[:, s0:s1].transpose(0, 3, 2, 1)),
            "st_pool_n": f(state_pool[:, s0:s1]), "st_conv_n": f(state_conv[:, s0:s1]),
            "st_sconv_n": f(state_sconv[:, s0:s1]),
        })
    nc = _get_nc()
    res = run_bass_kernel_spmd(nc, in_maps, core_ids=list(range(NCORES)))
    B_, DS = x_prompt.shape[0], x_sample.shape[0]
    y_prompt = np.empty((B_, SEQ, D), np.float32)
    y_sample = np.empty((DS, 1, D), np.float32)
    pool_p = np.empty((L, B_, 15, W_A), np.float32)
    pool_s = np.empty((L, DS, 15, W_A), np.float32)
    conv_p = np.empty((L, B_, 30, W_B), np.float32)
    conv_s = np.empty((L, DS, 30, W_B), np.float32)
    sconv_p = np.empty((L, B_, 2, W_C), np.float32)
    sconv_s = np.empty((L, DS, 2, W_C), np.float32)
    for c in range(NCORES):
        r = res.results[c]
        s0, s1 = c * NS, (c + 1) * NS
        yT = np.asarray(r["yT"])
        y_prompt[c] = yT[:, :SEQ].T
        y_sample[s0:s1, 0, :] = yT[:, SEQ:].T
        pt, ct, st = np.asarray(r["o_pool_t"]), np.asarray(r["o_conv_t"]), np.asarray(r["o_sconv_t"])
        pool_p[:, c] = pt[:, :, 0:15].transpose(0, 2, 1)
        pool_s[:, s0:s1, 0:14] = np.asarray(r["o_pool_old"])
        pool_s[:, s0:s1, 14] = pt[:, :, 15:31].transpose(0, 2, 1)
        conv_p[:, c] = ct[:, :, 0:30].transpose(0, 2, 1)
        conv_s[:, s0:s1, 0:29] = np.asarray(r["o_conv_old"])
        conv_s[:, s0:s1, 29] = ct[:, :, 30:46].transpose(0, 2, 1)
        sconv_p[:, c] = st[:, :, 0:2].transpose(0, 2, 1)
        sconv_s[:, s0:s1, 0:1] = np.asarray(r["o_sconv_old"])
        sconv_s[:, s0:s1, 1] = st[:, :, 2:18].transpose(0, 2, 1)
    return (y_prompt, y_sample, pool_p, pool_s, conv_p, conv_s, sconv_p, sconv_s)
```

```python
import contextlib
import numpy as np
import concourse.bass as bass
import concourse.mybir as mybir
from concourse.bass_utils import run_bass_kernel_spmd

F32 = mybir.dt.float32
BF16 = mybir.dt.bfloat16
AF = mybir.ActivationFunctionType
ALU = mybir.AluOpType

NCORES = 8
D = 1024
L = 2
SEQ = 2048
NS = 16
TT = SEQ + NS
TWS = [296] * 6 + [288]
NT = len(TWS)
TOFF = [sum(TWS[:i]) for i in range(NT)]
TWMAX = 296
W_A, W_B, W_C = 256, 384, 384
C_VA, C_ZA, C_AB, C_GB, C_ZB, C_XC, C_BC, C_CC, C_ZC = 0, 256, 512, 896, 1280, 1664, 2048, 2432, 2816
IN_COLS = 3200
EPS = 1e-6
WIN_GROUPS = [(C_GB, C_GB + 384), (C_AB, C_AB + 384), (C_VA, C_VA + 512), (C_XC, C_XC + 384), (C_CC, C_CC + 384),
              (C_ZC, C_ZC + 384), (C_BC, C_BC + 384), (C_ZB, C_ZB + 384)]
P_NG = 0
P_FG = P_NG + 16
P_PS = P_FG + 8
P_CW = P_PS + 4
P_CB = P_CW + 186
P_LG = P_CB + 6
P_LB = P_LG + 6
P_SW = P_LB + 6
NPRM = P_SW + 18
RING = 4
WARM_MM = 130
LAST_BLK = {}
for _i, (_g, _c) in enumerate([(0, C_GB), (0, C_GB + 128), (0, C_GB + 256), (1, C_AB), (1, C_AB + 128), (1, C_AB + 256),
                               (3, C_XC), (3, C_XC + 128), (3, C_XC + 256), (4, C_CC), (4, C_CC + 128), (4, C_CC + 256),
                               (2, C_VA), (2, C_VA + 128), (2, C_VA + 256), (2, C_VA + 384)]):
    LAST_BLK[_i] = (_g, _c)
RECIP_MODE = 0
B_RATIO = 1.0
OPT_BUILD = 'dve'
OPT_SKIPSELF = True


class Ev:
    __slots__ = ("sem", "val", "snap")

    def __init__(self, sem, val, snap):
        self.sem, self.val, self.snap = sem, val, snap


class Buf:
    __slots__ = ("name", "writer", "readers")

    def __init__(self, name):
        self.name, self.writer, self.readers = name, None, []


class Eng:
    def __init__(self, name, h, semname, is_pe=False):
        self.name, self.h, self.semname, self.is_pe = name, h, semname, is_pe
        self.count = 0
        self.seen = {}
        self._snap = None
        self.prog = []
        self.abs = []

    def snap(self):
        if self._snap is None:
            self._snap = dict(self.seen)
        return self._snap


class Tracker:
    def __init__(self):
        self.sems = {}
        self.semval = {}

    def add_sem(self, name, handle):
        self.sems[name] = handle
        self.semval[name] = 0

    def _waits(self, eng, raw, oth, skip_self=False):
        need = []
        for ev in raw:
            if ev.sem == eng.semname and eng.is_pe:
                continue
            need.append(ev)
        for ev in oth:
            if ev.sem == eng.semname and (eng.is_pe or skip_self):
                continue
            need.append(ev)
        need.sort(key=lambda e: -e.val)
        for ev in need:
            if eng.seen.get(ev.sem, 0) >= ev.val:
                continue
            eng.prog.append(lambda hh, s_=self.sems[ev.sem], v_=ev.val: hh.wait_ge(s_, v_))
            eng.abs.append(("wait", ev.sem, ev))
            eng.seen[ev.sem] = ev.val
            if ev.snap:
                for k, v in ev.snap.items():
                    if eng.seen.get(k, 0) < v:
                        eng.seen[k] = v
            eng._snap = None

    def op(self, eng, fn, reads=(), writes=(), final=True, skip_self=False):
        raw = [b.writer for b in reads if b.writer is not None]
        oth = []
        for b in reads:
            if b.name.startswith("bank"):
                oth.extend(r for r in b.readers if r.sem != eng.semname)
        for b in writes:
            if b.writer is not None:
                oth.append(b.writer)
            oth.extend(b.readers)
        self._waits(eng, raw, oth, skip_self and OPT_SKIPSELF)
        if final:
            eng.count += 1
            eng.prog.append(lambda hh, fn=fn, s_=self.sems[eng.semname]: fn(hh).then_inc(s_, 1))
            eng.abs.append(("inc", eng.semname, 1))
            val = eng.count
        else:
            eng.prog.append(lambda hh, fn=fn: fn(hh))
            val = eng.count + 1
        ev = Ev(eng.semname, val, eng.snap())
        for b in reads:
            b.readers.append(ev)
        for b in writes:
            b.writer = ev
            b.readers = []
        return None

    def dma(self, q, out_ap, in_ap, semname, reads=(), writes=(), group=None):
        raw = [b.writer for b in reads if b.writer is not None]
        oth = []
        for b in writes:
            if b.writer is not None:
                oth.append(b.writer)
            oth.extend(b.readers)
        self._waits(q, raw, oth)
        self.semval[semname] += 16
        q.prog.append(lambda hh, o_=out_ap, i_=in_ap, s_=self.sems[semname]: hh.dma_start(out=o_, in_=i_).then_inc(s_, 16))
        q.abs.append(("inc", semname, 16))
        ev = Ev(semname, self.semval[semname], q.snap())
        if group is not None:
            group.append(ev)
        for b in reads:
            b.readers.append(ev)
        for b in writes:
            b.writer = ev
            b.readers = []
        return ev

    def close_group(self, group, semname):
        tot = self.semval[semname]
        for ev in group:
            ev.val = tot


def check_deadlock(engs):
    val = {}
    pc = {e.name: 0 for e in engs}
    progress = True
    while progress:
        progress = False
        for e in engs:
            while pc[e.name] < len(e.abs):
                kind, sem, a = e.abs[pc[e.name]]
                if kind == "wait":
                    if val.get(sem, 0) >= a.val:
                        pc[e.name] += 1
                        progress = True
                    else:
                        break
                else:
                    val[sem] = val.get(sem, 0) + a
                    pc[e.name] += 1
                    progress = True
    stuck = {e.name: (pc[e.name], len(e.abs), e.abs[pc[e.name]][1], e.abs[pc[e.name]][2].val, val.get(e.abs[pc[e.name]][1], 0))
             for e in engs if pc[e.name] < len(e.abs)}
    assert not stuck, f"DEADLOCK in sync plan: {stuck}"


def build_nc():
    nc = bass.Bass("TRN2", target_bir_lowering=False)
    dt = nc.dram_tensor
    xT = dt("xT", [D, TT], F32, kind="ExternalInput").ap()
    peT = dt("peT", [L, 256, TT], F32, kind="ExternalInput").ap()
    w_in = dt("w_in", [L, D, IN_COLS], F32, kind="ExternalInput").ap()
    w_out = dt("w_out", [L, D, D], F32, kind="ExternalInput").ap()
    w_ple = dt("w_ple", [L, 256, D], F32, kind="ExternalInput").ap()
    w_gate = dt("w_gate", [L, D, D], F32, kind="ExternalInput").ap()
    w_pool = dt("w_pool", [L, 4, 64, 64], F32, kind="ExternalInput").ap()
    prm_d = dt("prm", [128, NPRM], F32, kind="ExternalInput").ap()
    st_pool_f = dt("st_pool_f", [L, 256, 15, NS], F32, kind="ExternalInput").ap()
    st_conv_f = dt("st_conv_f", [L, 384, 30, NS], F32, kind="ExternalInput").ap()
    st_sconv_f = dt("st_sconv_f", [L, 384, 2, NS], F32, kind="ExternalInput").ap()
    st_pool_n = dt("st_pool_n", [L, NS, 15, 256], F32, kind="ExternalInput").ap()
    st_conv_n = dt("st_conv_n", [L, NS, 30, 384], F32, kind="ExternalInput").ap()
    st_sconv_n = dt("st_sconv_n", [L, NS, 2, 384], F32, kind="ExternalInput").ap()
    yT = dt("yT", [D, TT], F32, kind="ExternalOutput").ap()
    o_pool_t = dt("o_pool_t", [L, 256, 31], F32, kind="ExternalOutput").ap()
    o_conv_t = dt("o_conv_t", [L, 384, 46], F32, kind="ExternalOutput").ap()
    o_sconv_t = dt("o_sconv_t", [L, 384, 18], F32, kind="ExternalOutput").ap()
    o_pool_old = dt("o_pool_old", [L, NS, 14, 256], F32, kind="ExternalOutput").ap()
    o_conv_old = dt("o_conv_old", [L, NS, 29, 384], F32, kind="ExternalOutput").ap()
    o_sconv_old = dt("o_sconv_old", [L, NS, 1, 384], F32, kind="ExternalOutput").ap()

    with contextlib.ExitStack() as es:
        def sb(name, shape, dtype):
            return es.enter_context(nc.sbuf_tensor(name, shape, dtype))

        x = sb("x", [128, 8, TT], F32)
        win = sb("win", [128, 8, IN_COLS], BF16)
        ring = sb("ring", [128, RING, 8, 128], BF16)
        wple = sb("wple", [128, 2, D], BF16)
        d31 = sb("d31", [128, 93, 128], BF16)
        d3 = sb("d3", [128, 9, 128], BF16)
        pm = sb("pm", [128, 6, 128], BF16)
        ones = sb("ones", [128, 128], BF16)
        ident = sb("ident", [128, 128], BF16)
        prm = sb("prm_s", [128, NPRM], F32)
        epsT = sb("epsT", [128, 1], F32)
        stage = sb("stage", [128, L, 2, 64], F32)
        rtab = sb("rtab", [128, 2, 15], F32)
        t15 = sb("t15", [128, 2, 15], F32)
        h = sb("h", [128, 2, 8, TWMAX], BF16)
        xb = sb("xb", [128, 8, TWMAX], BF16)
        ycat = sb("ycat", [128, 8, TWMAX], BF16)
        pe = sb("pe", [128, 2, 2, TWMAX], BF16)
        ub = sb("ub", [128, 3, 30 + TWMAX], BF16)
        ucb = sb("ucb", [128, 3, 2 + TWMAX], BF16)
        vb = sb("vb", [128, 2, 16 + TWMAX], BF16)
        ubs = sb("ubs", [128, 3, 31, NS], BF16)
        zsb = sb("zsb", [128, 2, 16, NS], BF16)
        scb = sb("scb", [128, 3, 3, NS], BF16)
        ubt = sb("ubt", [128, L, 3, 46], F32)
        vt = sb("vt", [128, L, 2, 31], F32)
        uct = sb("uct", [128, L, 3, 18], F32)
        sq = sb("sq", [128, 2, TWMAX], BF16)
        sg = sb("sg", [128, TWMAX], F32)
        cb = sb("cb", [128, 3, TWMAX], F32)
        cbb = sb("cbb", [128, 2, TWMAX], BF16)
        csq = sb("csq", [128, 2, TWMAX], BF16)
        msq = sb("msq", [128, TWMAX], F32)
        sa = sb("sa", [128, TWMAX], F32)
        mneg = sb("mneg", [128, TWMAX], F32)
        rsn = sb("rsn", [128, TWMAX], F32)
        xc = sb("xc", [128, TWMAX], F32)
        szc = sb("szc", [128, TWMAX], F32)
        tt = sb("tt", [128, TWMAX], F32)
        gt = sb("gt", [128, 2, TWMAX], F32)
        banks = [es.enter_context(nc.psum_tensor(f"ps{i}", [128, 512], F32)) for i in range(8)]

        TR = Tracker()

        def sem(name):
            s = es.enter_context(nc.semaphore(name))
            TR.add_sem(name, s)
            return s

        for nm in ["pe_s", "act_s", "dve_s", "pool_s", "prm_q", "out_q", "st_q", "wple_q", "setup_q"]:
            sem(nm)
        for n in range(NT):
            sem(f"x{n}")
        for i_ in range(3):
            sem(f"x0p{i_}")
        for g in range(len(WIN_GROUPS)):
            sem(f"win{g}")
        for s in range(RING):
            sem(f"ring{s}")
        for s in range(2):
            sem(f"pe{s}")

        PE = Eng("pe", nc.tensor, "pe_s", is_pe=True)
        ACT = Eng("act", nc.scalar, "act_s")
        DVE = Eng("dve", nc.vector, "dve_s")
        POOL = Eng("pool", nc.gpsimd, "pool_s")
        SP = Eng("sp", nc.sync, None)

        B = {}

        def buf(name):
            if name not in B:
                B[name] = Buf(name)
            return B[name]

        bank_buf = [Buf(f"bank{i}") for i in range(8)]
        bank_free_order = [0] * 8
        bank_busy = [False] * 8
        order = [0]

        def alloc_bank():
            best = None
            for i in range(8):
                if not bank_busy[i] and (best is None or bank_free_order[i] < bank_free_order[best]):
                    best = i
            assert best is not None, "out of PSUM banks"
            bank_busy[best] = True
            return best

        def release(i):
            order[0] += 1
            bank_busy[i] = False
            bank_free_order[i] = order[0]

        MMC = {"A": 0, "B": 0}
        CUR = ["A"]

        def mm(bi, c0, c1, lhsT, rhs, start, stop, reads, final):
            MMC[CUR[0]] += 1
            TR.op(PE, lambda t: t.matmul(banks[bi][:, c0:c1], lhsT=lhsT, rhs=rhs, start=start, stop=stop),
                  reads=reads, writes=[bank_buf[bi]], final=final)

        def pcol(i):
            return prm[:, i:i + 1]

        blocks = []
        for l in range(L):
            for n in range(NT):
                if l == L - 1 and n == NT - 1:
                    continue
                for m in range(8):
                    blocks.append((l, "o", m))
                for m in range(8):
                    blocks.append((l, "g", m))
        ring_issued = [0]

        def ring_issue():
            i = ring_issued[0]
            if i >= len(blocks):
                return
            l, kind, m = blocks[i]
            src = (w_out if kind == "o" else w_gate)[l].rearrange("(k p) n -> p k n", p=128)[:, :, m * 128:(m + 1) * 128]
            s = i % RING
            TR.dma(POOL, ring[:, s], src, f"ring{s}", writes=[buf(f"ring{s}")])
            ring_issued[0] += 1

        ring_used = [0]

        def ring_next():
            i = ring_used[0]
            assert i < ring_issued[0]
            ring_used[0] += 1
            return i % RING

        def plan():
            prm_grp = []
            TR.dma(SP, prm[:], prm_d, "prm_q", writes=[buf("prm")], group=prm_grp)
            TR.dma(SP, stage[:], w_pool.rearrange("l (j hh) c d -> (hh c) l j d", hh=2), "prm_q",
                   writes=[buf("stage")], group=prm_grp)
            TR.close_group(prm_grp, "prm_q")
            def load_x(n, gate=()):
                TR.dma(SP, x[:, :, TOFF[n]:TOFF[n] + TWS[n]],
                       xT.rearrange("(c p) t -> p c t", p=128)[:, :, TOFF[n]:TOFF[n] + TWS[n]], f"x{n}",
                       reads=list(gate), writes=[buf(f"x{c}_{n}") for c in range(8)])

            xv0 = xT.rearrange("(c p) t -> p c t", p=128)
            for i_ in range(4):
                TR.dma(SP, x[:, 2 * i_:2 * i_ + 2, 0:TWS[0]], xv0[:, 2 * i_:2 * i_ + 2, 0:TWS[0]],
                       "x0" if i_ == 0 else f"x0p{i_ - 1}", writes=[buf(f"x{c}_0") for c in (2 * i_, 2 * i_ + 1)])
            out_grp = []
            for l in range(L):
                TR.dma(SP, o_pool_old[l], st_pool_n[l, :, 1:15, :], "out_q", group=out_grp)
                TR.dma(SP, o_conv_old[l], st_conv_n[l, :, 1:30, :], "out_q", group=out_grp)
                TR.dma(SP, o_sconv_old[l], st_sconv_n[l, :, 1:2, :], "out_q", group=out_grp)

            def load_win(l):
                for g, (c0, c1) in enumerate(WIN_GROUPS):
                    TR.dma(POOL, win[:, :, c0:c1], w_in[l].rearrange("(k p) n -> p k n", p=128)[:, :, c0:c1],
                           f"win{g}", writes=[buf(f"win{g}")])

            def load_win_group(l, g, gate=()):
                c0, c1 = WIN_GROUPS[g]
                TR.dma(POOL, win[:, :, c0:c1], w_in[l].rearrange("(k p) n -> p k n", p=128)[:, :, c0:c1],
                       f"win{g}", reads=list(gate), writes=[buf(f"win{g}")])

            def load_last_blocks(group):
                grp = []
                first = True
                for i_, (g_, c_) in LAST_BLK.items():
                    if g_ != group:
                        continue
                    kind, m_ = ("o", i_) if i_ < 8 else ("g", i_ - 8)
                    src = (w_out if kind == "o" else w_gate)[L - 1].rearrange("(k p) n -> p k n", p=128)[:, :, m_ * 128:(m_ + 1) * 128]
                    wr = [buf(f"lb{i_}")] + ([buf(f"win{g_}")] if first else [])
                    TR.dma(POOL, win[:, :, c_:c_ + 128], src, f"win{g_}", writes=wr, group=grp)
                    first = False
                TR.close_group(grp, f"win{group}")

            def load_wple(l):
                TR.dma(POOL, wple[:], w_ple[l].rearrange("(k p) n -> p k n", p=128), "wple_q", writes=[buf("wple")])

            def load_states(l):
                grp = []
                TR.dma(POOL, ubs[:, :, 0:30, :], st_conv_f[l].rearrange("(j p) r s -> p j r s", p=128), "st_q",
                       writes=[buf("ubs")], group=grp)
                TR.dma(POOL, zsb[:, :, 0:15, :], st_pool_f[l].rearrange("(j p) r s -> p j r s", p=128), "st_q",
                       writes=[buf("zsb")], group=grp)
                TR.dma(POOL, scb[:, :, 0:2, :], st_sconv_f[l].rearrange("(j p) r s -> p j r s", p=128), "st_q",
                       writes=[buf("scb")], group=grp)
                TR.close_group(grp, "st_q")

            def build_d31(l, j, eng=None):
                eng = eng or POOL
                for k in range(31):
                    idx = j * 31 + k
                    sc = pcol(P_CW + l * 93 + idx)
                    rd, wr = [buf("ident"), buf("prm")], [buf(f"d31_{j}")]
                    if eng is POOL:
                        TR.op(POOL, lambda g, idx=idx, sc=sc: g.tensor_scalar(
                            out=d31[:, idx, :], in0=ident[:], scalar1=sc, scalar2=0.0,
                            op0=ALU.mult, op1=ALU.add), reads=rd, writes=wr, skip_self=True)
                    elif eng is DVE:
                        TR.op(DVE, lambda v, idx=idx, sc=sc: v.tensor_scalar(
                            out=d31[:, idx, :], in0=ident[:], scalar1=sc, scalar2=None, op0=ALU.mult),
                            reads=rd, writes=wr, skip_self=True)
                    else:
                        TR.op(ACT, lambda a, idx=idx, sc=sc: a.activation(
                            out=d31[:, idx, :], in_=ident[:], func=AF.Copy, scale=sc), reads=rd, writes=wr, skip_self=True)

            def build_d3(l):
                for j in range(3):
                    for k in range(3):
                        idx = j * 3 + k
                        TR.op(POOL, lambda g, idx=idx, l=l: g.tensor_scalar(
                            out=d3[:, idx, :], in0=ident[:], scalar1=pcol(P_SW + l * 9 + idx), scalar2=0.0,
                            op0=ALU.mult, op1=ALU.add), reads=[buf("ident"), buf("prm")], writes=[buf(f"d3_{j}")], skip_self=True)

            def build_pm(l):
                for j in range(2):
                    w_lo, w_hi = ((2, 4), (8, 16))[j]
                    specs = [(0, 64, j * 3 + 0, 1.0 / w_lo), (64, 128, j * 3 + 0, 1.0 / w_hi),
                             (64, 128, j * 3 + 1, 1.0 / w_hi), (0, 64, j * 3 + 2, -1.0), (64, 128, j * 3 + 2, -1.0)]
                    for (p0, p1, mi, sc) in specs:
                        TR.op(POOL, lambda g, p0=p0, p1=p1, mi=mi, sc=sc, l=l, j=j: g.tensor_scalar(
                            out=pm[p0:p1, mi, p0:p1], in0=stage[p0:p1, l, j, :], scalar1=float(sc), scalar2=0.0,
                            op0=ALU.mult, op1=ALU.add), reads=[buf("stage")], writes=[buf(f"pm_{j}")])

            def build_diags(l):
                build_d3(l)

            TR.op(DVE, lambda v: v.memset(ones[:], 1.0), writes=[buf("ones")])
            TR.op(DVE, lambda v: v.memset(epsT[:], EPS), writes=[buf("eps")])
            TR.op(ACT, lambda a: a.activation(out=t15[:, 0, 0:1], in_=epsT[:], func=AF.Sqrt), reads=[buf("eps")], writes=[buf("t15_0")])
            load_win_group(0, 0, gate=[buf("x1_0")])
            load_win_group(0, 1)
            TR.op(POOL, lambda g: g.affine_select(out=ident[:], in_=ones[:], pattern=[[1, 128]], compare_op=ALU.is_equal,
                                                  fill=0.0, base=0, channel_multiplier=-1),
                  reads=[buf("ones")], writes=[buf("ident")])
            for g_ in range(2, 5):
                load_win_group(0, g_)
            load_x(1, gate=[buf("win3")])
            build_d31(0, 1, POOL)
            build_d31(0, 2, POOL)
            TR.op(POOL, lambda g: g.memset(pm[:], 0.0), writes=[buf("pm_0"), buf("pm_1")])
            build_pm(0)
            for g_ in range(5, len(WIN_GROUPS)):
                load_win_group(0, g_)
            load_wple(0)
            for _i in range(RING):
                ring_issue()
            load_states(0)
            TR.op(POOL, lambda g: g.memset(rtab[:], 1.0), writes=[buf("rtab")])
            for j in range(2):
                for hh in range(2):
                    w = ((2, 4), (8, 16))[j][hh]
                    for t in range(w - 1):
                        TR.op(POOL, lambda g, j=j, hh=hh, t=t, w=w: g.memset(rtab[hh * 64:(hh + 1) * 64, j, t:t + 1],
                                                                          float(w) / (t + 1)), writes=[buf("rtab")])
            build_diags(0)

            def geom(n):
                Wn = TWS[n]
                last = (n == NT - 1)
                PW = Wn - NS if last else Wn
                return Wn, TOFF[n], PW, last

            def win_group_of(col):
                for g, (c0, c1) in enumerate(WIN_GROUPS):
                    if c0 <= col < c1:
                        return g
                raise AssertionError

            def inproj(col0, Wn, hb):
                bi = alloc_bank()
                g = win_group_of(col0)
                for k in range(8):
                    mm(bi, 0, Wn, win[:, k, col0:col0 + 128], h[:, hb, k, :Wn], k == 0, k == 7,
                       [buf(f"win{g}"), buf(f"h{hb}")], k == 7)
                return bi

            def recip(bi, Wn):
                bb = bank_buf[bi]
                if RECIP_MODE == 0:
                    TR.op(DVE, lambda v: v.reciprocal(out=banks[bi][:, :Wn], in_=banks[bi][:, :Wn]), reads=[bb], writes=[bb])
                else:
                    TR.op(DVE, lambda v: v.reciprocal_approx_accurate(out=banks[bi][:, :Wn], in_=banks[bi][:, :Wn],
                                                                      scratch=rscr[:, :Wn]),
                          reads=[bb], writes=[bb, buf("rscr")])

            def norm_head(n, scr, scr_bufs):
                Wn, On, PW, last = geom(n)
                cs = slice(On, On + Wn)
                for c in range(8):
                    TR.op(ACT, lambda a, c=c: a.activation(out=scr(c)[:, :Wn], in_=x[:, c, cs], func=AF.Square),
                          reads=[buf(f"x{c}_{n}")], writes=[scr_bufs[c]])
                bS = alloc_bank()
                for c in range(8):
                    mm(bS, 0, Wn, ones[:], scr(c)[:, :Wn], c == 0, c == 7, [buf("ones"), scr_bufs[c]], c == 7)
                return bS

            def norm_sqrt(bS, n):
                Wn = TWS[n]
                bb = bank_buf[bS]
                TR.op(ACT, lambda a: a.activation(out=rsn[:, :Wn], in_=banks[bS][:, :Wn], func=AF.Sqrt,
                                                  bias=epsT[:], scale=1.0 / D), reads=[bb, buf("eps")], writes=[buf("rsn")])
                release(bS)

            def norm_recip(n):
                Wn = TWS[n]
                TR.op(DVE, lambda v: v.reciprocal(out=rsn[:, :Wn], in_=rsn[:, :Wn]), reads=[buf("rsn")], writes=[buf("rsn")])

            def norm_apply(l, n, bS, hb, final_norm):
                Wn, On, PW, last = geom(n)
                cs = slice(On, On + Wn)
                for c in range(8):
                    if final_norm:
                        TR.op(DVE, lambda v, c=c: v.scalar_tensor_tensor(
                            out=x[:, c, cs], in0=x[:, c, cs], scalar=pcol(P_FG + c), in1=rsn[:, :Wn],
                            op0=ALU.mult, op1=ALU.mult), reads=[buf(f"x{c}_{n}"), buf("rsn"), buf("prm")], writes=[buf(f"x{c}_{n}")])
                    else:
                        TR.op(DVE, lambda v, c=c: v.scalar_tensor_tensor(
                            out=h[:, hb, c, :Wn], in0=x[:, c, cs], scalar=pcol(P_NG + l * 8 + c), in1=rsn[:, :Wn],
                            op0=ALU.mult, op1=ALU.mult), reads=[buf(f"x{c}_{n}"), buf("rsn"), buf("prm")], writes=[buf(f"h{hb}")])

            def rstd_in_psum(bS, Wn):
                bb = bank_buf[bS]
                TR.op(ACT, lambda a: a.activation(out=banks[bS][:, :Wn], in_=banks[bS][:, :Wn], func=AF.Sqrt,
                                                  bias=epsT[:], scale=1.0 / D), reads=[bb, buf("eps")], writes=[bb])
                TR.op(DVE, lambda v: v.reciprocal(out=banks[bS][:, :Wn], in_=banks[bS][:, :Wn]), reads=[bb], writes=[bb])

            def norm_full(l, n, hb, final_norm=False):
                Wn, On, PW, last = geom(n)
                cs = slice(On, On + Wn)
                bS = norm_head(n, lambda c: h[:, hb, c, :], [buf(f"h{hb}")] * 8)
                bw = alloc_bank()
                for i_ in range(WARM_MM):
                    mm(bw, 0, 128, ones[:], ident[:], True, True, [buf("ones"), buf("ident")], i_ == WARM_MM - 1)
                release(bw)
                rstd_in_psum(bS, Wn)
                for c in range(8):
                    TR.op(DVE, lambda v, c=c: v.scalar_tensor_tensor(
                        out=h[:, hb, c, :Wn], in0=x[:, c, cs], scalar=pcol(P_NG + l * 8 + c), in1=banks[bS][:, :Wn],
                        op0=ALU.mult, op1=ALU.mult), reads=[buf(f"x{c}_{n}"), bank_buf[bS], buf("prm")], writes=[buf(f"h{hb}")])
                release(bS)

            def A_tile(l, n):
                g = l * NT + n
                hb = g % 2
                Wn, On, PW, last = geom(n)
                Wp = TWMAX
                boundary = last and (l + 1 < L)
                slot = g % 2
                if l == 0 and n + 2 < NT:
                    load_x(n + 2, gate=[buf(f"h{hb}")])
                TR.dma(POOL, pe[:, slot, :, :Wn], peT[l].rearrange("(k p) t -> p k t", p=128)[:, :, On:On + Wn],
                       f"pe{slot}", writes=[buf(f"pe{slot}")])
                for j in range(3):
                    bg = inproj(C_GB + 128 * j, Wn, hb)
                    TR.op(ACT, lambda a, bg=bg: a.activation(out=sg[:, :Wn], in_=banks[bg][:, :Wn], func=AF.Tanh, scale=0.5),
                          reads=[bank_buf[bg]], writes=[buf("sg")])
                    release(bg)
                    TR.op(DVE, lambda v: v.tensor_scalar(out=sg[:, :Wn], in0=sg[:, :Wn], scalar1=0.5, scalar2=0.5,
                                                         op0=ALU.mult, op1=ALU.add), reads=[buf("sg")], writes=[buf("sg")])
                    yield
                    ba = inproj(C_AB + 128 * j, Wn, hb)
                    if n == 0:
                        TR.op(DVE, lambda v, j=j: v.memset(ub[:, j, 0:30], 0.0), writes=[buf(f"ub{j}")])
                    else:
                        TR.op(DVE, lambda v, j=j: v.tensor_copy(out=ub[:, j, 0:30], in_=ub[:, j, Wp:Wp + 30]),
                              reads=[buf(f"ub{j}")], writes=[buf(f"ub{j}")])
                    TR.op(DVE, lambda v, j=j, ba=ba: v.tensor_tensor(out=ub[:, j, 30:30 + Wn], in0=banks[ba][:, :Wn],
                                                                     in1=sg[:, :Wn], op=ALU.mult),
                          reads=[bank_buf[ba], buf("sg")], writes=[buf(f"ub{j}")])
                    if last:
                        TR.op(DVE, lambda v, j=j, ba=ba: v.tensor_tensor(out=ubt[:, l, j, :], in0=banks[ba][:, PW - 30:Wn],
                                                                         in1=sg[:, PW - 30:Wn], op=ALU.mult),
                              reads=[bank_buf[ba], buf("sg")], writes=[buf(f"ubt{l}")])
                        TR.op(DVE, lambda v, j=j: v.tensor_copy(out=ubs[:, j, 30, :], in_=ub[:, j, 30 + PW:30 + Wn]),
                              reads=[buf(f"ub{j}")], writes=[buf("ubs")])
                    release(ba)
                    yield
                if boundary:
                    load_win_group(l + 1, 0)
                    load_win_group(l + 1, 1)
                if l == L - 1 and last:
                    load_last_blocks(0)
                    load_last_blocks(1)
                if l >= 1 and n == 0:
                    load_win_group(l, 5)
                    load_win_group(l, 6)
                    load_win_group(l, 7)
                def c_part1(j):
                    bx = inproj(C_XC + 128 * j, Wn, hb)
                    TR.op(ACT, lambda a, bx=bx: a.activation(out=xc[:, :Wn], in_=banks[bx][:, :Wn], func=AF.Copy),
                          reads=[bank_buf[bx]], writes=[buf("xc")])
                    release(bx)
                    yield 0
                    bc = inproj(C_CC + 128 * j, Wn, hb)
                    if n == 0:
                        TR.op(DVE, lambda v, j=j: v.memset(ucb[:, j, 0:2], 0.0), writes=[buf(f"ucb{j}")])
                    else:
                        TR.op(DVE, lambda v, j=j: v.tensor_copy(out=ucb[:, j, 0:2], in_=ucb[:, j, Wp:Wp + 2]),
                              reads=[buf(f"ucb{j}")], writes=[buf(f"ucb{j}")])
                    TR.op(DVE, lambda v, j=j, bc=bc: v.tensor_tensor(out=ucb[:, j, 2:2 + Wn], in0=banks[bc][:, :Wn],
                                                                     in1=xc[:, :Wn], op=ALU.mult),
                          reads=[bank_buf[bc], buf("xc")], writes=[buf(f"ucb{j}")])
                    if last:
                        TR.op(DVE, lambda v, j=j, bc=bc: v.tensor_tensor(out=uct[:, l, j, :], in0=banks[bc][:, PW - 2:Wn],
                                                                         in1=xc[:, PW - 2:Wn], op=ALU.mult),
                              reads=[bank_buf[bc], buf("xc")], writes=[buf(f"uct{l}")])
                        TR.op(DVE, lambda v, j=j: v.tensor_copy(out=scb[:, j, 2, :], in_=ucb[:, j, 2 + PW:2 + Wn]),
                              reads=[buf(f"ucb{j}")], writes=[buf("scb")])
                    release(bc)
                    yield 1

                def va_chunk(j):
                    bv = inproj(C_VA + 128 * j, Wn, hb)
                    if n == 0:
                        TR.op(DVE, lambda v, j=j: v.memset(vb[:, j, 0:16], 0.0), writes=[buf(f"vb{j}")])
                    else:
                        TR.op(DVE, lambda v, j=j: v.tensor_copy(out=vb[:, j, 1:16], in_=vb[:, j, Wp + 1:Wp + 16]),
                              reads=[buf(f"vb{j}")], writes=[buf(f"vb{j}")])
                    TR.op(ACT, lambda a, j=j, bv=bv: a.activation(out=vb[:, j, 16:16 + Wn], in_=banks[bv][:, :Wn], func=AF.Copy),
                          reads=[bank_buf[bv]], writes=[buf(f"vb{j}")])
                    if last:
                        TR.op(ACT, lambda a, j=j, bv=bv: a.activation(out=vt[:, l, j, :], in_=banks[bv][:, PW - 15:Wn], func=AF.Copy),
                              reads=[bank_buf[bv]], writes=[buf(f"vt{l}")])
                        TR.op(ACT, lambda a, j=j, bv=bv: a.activation(out=zsb[:, j, 15, :], in_=banks[bv][:, PW:Wn], func=AF.Copy),
                              reads=[bank_buf[bv]], writes=[buf("zsb")])
                    release(bv)

                bsx = {}

                def stats(j):
                    if j == 0:
                        bsx[1] = alloc_bank()
                        bsx[2] = alloc_bank()
                    mm(bsx[1], 0, Wn, ones[:], cbb[:, j % 2, :Wn], j == 0, j == 2, [buf("ones"), buf(f"cbb{j % 2}")], True)
                    mm(bsx[2], 0, Wn, ones[:], csq[:, j % 2, :Wn], j == 0, j == 2, [buf("ones"), buf(f"csq{j % 2}")], True)

                for j in range(3):
                    bcv = alloc_bank()
                    for k in range(31):
                        mm(bcv, 0, PW, d31[:, j * 31 + k, :], ub[:, j, k:k + PW], k == 0, k == 30,
                           [buf(f"d31_{j}"), buf(f"ub{j}")], (k == 30 and not last))
                    if last:
                        for k in range(31):
                            mm(bcv, PW, Wn, d31[:, j * 31 + k, :], ubs[:, j, k, :], k == 0, k == 30,
                               [buf(f"d31_{j}"), buf("ubs")], k == 30)
                    bias = pcol(P_CB + l * 3 + j)
                    TR.op(ACT, lambda a, j=j, bcv=bcv, bias=bias: a.activation(
                        out=cb[:, j, :Wn], in_=banks[bcv][:, :Wn], func=AF.Identity, bias=bias),
                        reads=[bank_buf[bcv], buf("prm")], writes=[buf(f"cb{j}")])
                    TR.op(ACT, lambda a, j=j, bcv=bcv, bias=bias: a.activation(
                        out=cbb[:, j % 2, :Wn], in_=banks[bcv][:, :Wn], func=AF.Identity, bias=bias),
                        reads=[bank_buf[bcv], buf("prm")], writes=[buf(f"cbb{j % 2}")])
                    TR.op(ACT, lambda a, j=j, bcv=bcv, bias=bias: a.activation(
                        out=csq[:, j % 2, :Wn], in_=banks[bcv][:, :Wn], func=AF.Square, bias=bias),
                        reads=[bank_buf[bcv], buf("prm")], writes=[buf(f"csq{j % 2}")])
                    release(bcv)
                    if boundary:
                        build_d31(l + 1, j, DVE)
                    if j >= 1:
                        stats(j - 1)
                    yield

                nxt = (l, n + 1) if n + 1 < NT else ((l + 1, 0) if l + 1 < L else None)
                st = {}
                chain = []

                hn = (g + 1) % 2
                has_n = nxt is not None

                def c_mneg():
                    bs1 = bsx[1]
                    TR.op(DVE, lambda v: v.tensor_scalar(out=mneg[:, :Wn], in0=banks[bs1][:, :Wn], scalar1=-1.0 / W_B,
                                                         scalar2=None, op0=ALU.mult), reads=[bank_buf[bs1]], writes=[buf("mneg")])
                    release(bs1)

                def c_msq():
                    TR.op(ACT, lambda a: a.activation(out=msq[:, :Wn], in_=mneg[:, :Wn], func=AF.Square),
                          reads=[buf("mneg")], writes=[buf("msq")])

                def c_var():
                    bs2 = bsx[2]
                    TR.op(DVE, lambda v: v.scalar_tensor_tensor(out=msq[:, :Wn], in0=banks[bs2][:, :Wn], scalar=1.0 / W_B,
                                                                in1=msq[:, :Wn], op0=ALU.mult, op1=ALU.subtract),
                          reads=[bank_buf[bs2], buf("msq")], writes=[buf("msq")])
                    release(bs2)

                def c_nsq(ca=0, cz=8):
                    if has_n:
                        n2 = nxt[1]
                        W2, O2 = TWS[n2], TOFF[n2]
                        for c in range(ca, cz):
                            TR.op(ACT, lambda a, c=c: a.activation(out=h[:, hn, c, :W2], in_=x[:, c, O2:O2 + W2], func=AF.Square),
                                  reads=[buf(f"x{c}_{n2}")], writes=[buf(f"h{hn}")])

                def c_nmm():
                    if has_n:
                        n2 = nxt[1]
                        W2 = TWS[n2]
                        bS = alloc_bank()
                        for c in range(8):
                            mm(bS, 0, W2, ones[:], h[:, hn, c, :W2], c == 0, c == 7, [buf("ones"), buf(f"h{hn}")], c == 7)
                        st["bS"] = bS

                def c_sqrts():
                    TR.op(ACT, lambda a: a.activation(out=msq[:, :Wn], in_=msq[:, :Wn], func=AF.Relu),
                          reads=[buf("msq")], writes=[buf("msq")])
                    TR.op(ACT, lambda a: a.activation(out=msq[:, :Wn], in_=msq[:, :Wn], func=AF.Sqrt, bias=epsT[:],
                                                      scale=1.0), reads=[buf("msq"), buf("eps")], writes=[buf("msq")])
                    if has_n:
                        norm_sqrt(st["bS"], nxt[1])

                def c_rec_ln():
                    TR.op(DVE, lambda v: v.reciprocal(out=msq[:, :Wn], in_=msq[:, :Wn]), reads=[buf("msq")], writes=[buf("msq")])

                def c_rec_n():
                    if has_n:
                        norm_recip(nxt[1])

                def c_ln_dve(j):
                    cj = buf(f"cb{j}")
                    TR.op(DVE, lambda v: v.tensor_tensor(out=cb[:, j, :Wn], in0=cb[:, j, :Wn], in1=mneg[:, :Wn],
                                                         op=ALU.add), reads=[cj, buf("mneg")], writes=[cj])
                    TR.op(DVE, lambda v: v.tensor_tensor(out=cb[:, j, :Wn], in0=cb[:, j, :Wn], in1=msq[:, :Wn],
                                                         op=ALU.mult), reads=[cj, buf("msq")], writes=[cj])

                def c_ln_act(j):
                    cj = buf(f"cb{j}")
                    TR.op(ACT, lambda a: a.activation(out=cb[:, j, :Wn], in_=cb[:, j, :Wn], func=AF.Silu,
                                                      bias=pcol(P_LB + l * 3 + j), scale=pcol(P_LG + l * 3 + j)),
                          reads=[cj, buf("prm")], writes=[cj])

                def c_napply(c0, c1):
                    if has_n:
                        n2 = nxt[1]
                        W2, O2 = TWS[n2], TOFF[n2]
                        for c in range(c0, c1):
                            TR.op(DVE, lambda v, c=c: v.scalar_tensor_tensor(
                                out=h[:, hn, c, :W2], in0=x[:, c, O2:O2 + W2], scalar=pcol(P_NG + nxt[0] * 8 + c), in1=rsn[:, :W2],
                                op0=ALU.mult, op1=ALU.mult), reads=[buf(f"x{c}_{n2}"), buf("rsn"), buf("prm")], writes=[buf(f"h{hn}")])

                chain += [lambda: (c_mneg(), c_nsq(0, 3)), lambda: (c_msq(), c_nsq(3, 6)), lambda: (c_var(), c_nsq(6, 8)),
                          lambda: None, c_nmm, lambda: None, c_sqrts, c_rec_ln, c_rec_n,
                          lambda: c_ln_dve(0), lambda: c_ln_dve(1), lambda: (c_ln_dve(2), c_ln_act(0)),
                          lambda: (c_napply(0, 2), c_ln_act(1)), lambda: (c_napply(2, 4), c_ln_act(2)),
                          lambda: c_napply(4, 6), lambda: c_napply(6, 8)]

                def drip(k=1):
                    for _ in range(k):
                        if chain:
                            chain.pop(0)()

                if l >= 1 and n == 0:
                    build_pm(l)
                va_chunk(0)
                stats(2)
                yield
                va_chunk(1)
                drip()
                yield
                for j in range(3):
                    for _half in c_part1(j):
                        drip()
                        yield
                if boundary:
                    load_win_group(l + 1, 3)
                    load_win_group(l + 1, 4)
                if l >= 1 and n == 0:
                    build_d3(l)
                if l >= 1 and n == 1:
                    load_states(l)
                if l == L - 1 and last:
                    load_last_blocks(3)
                    load_last_blocks(4)
                bfix = None
                for j in range(2):
                    bz = inproj(C_ZA + 128 * j, Wn, hb)
                    TR.op(ACT, lambda a, j=j, bz=bz: a.activation(out=sa[:, :Wn], in_=banks[bz][:, :Wn], func=AF.Silu),
                          reads=[bank_buf[bz]], writes=[buf("sa")])
                    release(bz)
                    drip()
                    yield
                    w_lo, w_hi = ((2, 4), (8, 16))[j]
                    bm = alloc_bank()
                    rd = [buf(f"pm_{j}"), buf(f"vb{j}")]
                    for d in range(w_hi):
                        mi = j * 3 + (0 if d < w_lo else 1)
                        mm(bm, 0, PW, pm[:, mi, :], vb[:, j, 16 - d:16 - d + PW], d == 0, False, rd, False)
                    mm(bm, 0, PW, pm[:, j * 3 + 2, :], vb[:, j, 16:16 + PW], False, True, rd, not last)
                    if last:
                        rd2 = [buf(f"pm_{j}"), buf("zsb")]
                        for d in range(w_hi):
                            mi = j * 3 + (0 if d < w_lo else 1)
                            mm(bm, PW, Wn, pm[:, mi, :], zsb[:, j, 15 - d, :], d == 0, False, rd2, False)
                        mm(bm, PW, Wn, pm[:, j * 3 + 2, :], zsb[:, j, 15, :], False, True, rd2, True)
                    if n == 0:
                        if bfix is None:
                            bfix = alloc_bank()
                        c0 = j * 64
                        for d in range(w_hi):
                            mi = j * 3 + (0 if d < w_lo else 1)
                            mm(bfix, c0, c0 + 15, pm[:, mi, :], vb[:, j, 16 - d:31 - d], d == 0, d == w_hi - 1, rd, False)
                        mm(bfix, c0 + 16, c0 + 31, pm[:, j * 3 + 2, :], vb[:, j, 16:31], True, True, rd, True)
                    TR.op(DVE, lambda v, j=j, bm=bm: v.scalar_tensor_tensor(
                        out=ycat[:, j, :Wn], in0=banks[bm][:, :Wn], scalar=pcol(P_PS + l * 2 + j), in1=sa[:, :Wn],
                        op0=ALU.mult, op1=ALU.mult), reads=[bank_buf[bm], buf("sa"), buf("prm")], writes=[buf(f"ycat{j}")])
                    release(bm)
                    if n == 0:
                        c0 = j * 64
                        fb = bank_buf[bfix]
                        TR.op(DVE, lambda v, j=j, c0=c0: v.tensor_tensor(out=t15[:, j, :], in0=banks[bfix][:, c0:c0 + 15],
                                                                         in1=rtab[:, j, :], op=ALU.mult),
                              reads=[fb, buf("rtab")], writes=[buf(f"t15_{j}")])
                        TR.op(DVE, lambda v, j=j, c0=c0: v.tensor_tensor(out=t15[:, j, :], in0=t15[:, j, :],
                                                                         in1=banks[bfix][:, c0 + 16:c0 + 31], op=ALU.add),
                              reads=[fb, buf(f"t15_{j}")], writes=[buf(f"t15_{j}")])
                        TR.op(DVE, lambda v, j=j: v.scalar_tensor_tensor(
                            out=ycat[:, j, 0:15], in0=t15[:, j, :], scalar=pcol(P_PS + l * 2 + j), in1=sa[:, 0:15],
                            op0=ALU.mult, op1=ALU.mult), reads=[buf(f"t15_{j}"), buf("sa"), buf("prm")],
                            writes=[buf(f"ycat{j}")])
                    drip()
                    yield
                if n == 0:
                    release(bfix)
                if boundary:
                    load_win_group(l + 1, 2)
                if l == L - 1 and last:
                    load_last_blocks(2)
                for j in range(3):
                    bzc = inproj(C_ZC + 128 * j, Wn, hb)
                    TR.op(ACT, lambda a, bzc=bzc: a.activation(out=szc[:, :Wn], in_=banks[bzc][:, :Wn], func=AF.Silu),
                          reads=[bank_buf[bzc]], writes=[buf("szc")])
                    release(bzc)
                    drip()
                    yield
                    bbc = inproj(C_BC + 128 * j, Wn, hb)
                    TR.op(DVE, lambda v, bbc=bbc: v.tensor_tensor(out=tt[:, :Wn], in0=banks[bbc][:, :Wn], in1=szc[:, :Wn],
                                                                  op=ALU.mult), reads=[bank_buf[bbc], buf("szc")], writes=[buf("tt")])
                    release(bbc)
                    bsc = alloc_bank()
                    for k in range(3):
                        mm(bsc, 0, PW, d3[:, j * 3 + k, :], ucb[:, j, k:k + PW], k == 0, k == 2,
                           [buf(f"d3_{j}"), buf(f"ucb{j}")], (k == 2 and not last))
                    if last:
                        for k in range(3):
                            mm(bsc, PW, Wn, d3[:, j * 3 + k, :], scb[:, j, k, :], k == 0, k == 2,
                               [buf(f"d3_{j}"), buf("scb")], k == 2)
                    TR.op(DVE, lambda v, j=j, bsc=bsc: v.tensor_tensor(out=ycat[:, 5 + j, :Wn], in0=banks[bsc][:, :Wn],
                                                                       in1=tt[:, :Wn], op=ALU.mult),
                          reads=[bank_buf[bsc], buf("tt")], writes=[buf(f"ycat{5 + j}")])
                    release(bsc)
                    drip()
                    yield
                if boundary:
                    pass
                drip(len(chain))
                for j in range(3):
                    bz = inproj(C_ZB + 128 * j, Wn, hb)
                    zb_ = bank_buf[bz]
                    TR.op(ACT, lambda a, bz=bz: a.activation(out=banks[bz][:, :Wn], in_=banks[bz][:, :Wn], func=AF.Silu),
                          reads=[zb_], writes=[zb_])
                    TR.op(DVE, lambda v, j=j, bz=bz: v.tensor_tensor(out=ycat[:, 2 + j, :Wn], in0=cb[:, j, :Wn],
                                                                     in1=banks[bz][:, :Wn], op=ALU.mult),
                          reads=[zb_, buf(f"cb{j}")], writes=[buf(f"ycat{2 + j}")])
                    release(bz)
                    yield

            def B_tile(l, n):
                g = l * NT + n
                Wn, On, PW, last = geom(n)
                cs = slice(On, On + Wn)
                slot = g % 2
                ycs = [buf(f"ycat{k}") for k in range(8)]
                very_last = (l == L - 1 and n == NT - 1)
                for m in range(8):
                    bo = alloc_bank()
                    KORD = [0, 1, 5, 6, 7, 2, 3, 4]
                    if very_last:
                        _g, _c = LAST_BLK[m]
                        for ki, k in enumerate(KORD):
                            mm(bo, 0, Wn, win[:, k, _c:_c + 128], ycat[:, k, :Wn], ki == 0, ki == 7, [buf(f"lb{m}"), ycs[k]], ki == 7)
                    else:
                        rs = ring_next()
                        for ki, k in enumerate(KORD):
                            mm(bo, 0, Wn, ring[:, rs, k, :], ycat[:, k, :Wn], ki == 0, ki == 7, [buf(f"ring{rs}"), ycs[k]], ki == 7)
                        ring_issue()
                    xm = buf(f"x{m}_{n}")
                    TR.op(DVE, lambda v, m=m, bo=bo: v.tensor_tensor(out=x[:, m, cs], in0=banks[bo][:, :Wn], in1=x[:, m, cs],
                                                                     op=ALU.add), reads=[bank_buf[bo], xm], writes=[xm])
                    release(bo)
                    TR.op(ACT, lambda a, m=m: a.activation(out=xb[:, m, :Wn], in_=x[:, m, cs], func=AF.Copy),
                          reads=[xm], writes=[buf(f"xb{m}")])
                    yield
                for m in range(8):
                    bg = alloc_bank()
                    if very_last:
                        _g, _c = LAST_BLK[8 + m]
                        for k in range(8):
                            mm(bg, 0, Wn, win[:, k, _c:_c + 128], xb[:, k, :Wn], k == 0, k == 7, [buf(f"lb{8 + m}"), buf(f"xb{k}")], k == 7)
                    else:
                        rs = ring_next()
                        for k in range(8):
                            mm(bg, 0, Wn, ring[:, rs, k, :], xb[:, k, :Wn], k == 0, k == 7, [buf(f"ring{rs}"), buf(f"xb{k}")], k == 7)
                        ring_issue()
                    bp = alloc_bank()
                    for k in range(2):
                        mm(bp, 0, Wn, wple[:, k, m * 128:(m + 1) * 128], pe[:, slot, k, :Wn], k == 0, k == 1,
                           [buf("wple"), buf(f"pe{slot}")], k == 1)
                    TR.op(ACT, lambda a, m=m, bg=bg: a.activation(out=gt[:, m % 2, :Wn], in_=banks[bg][:, :Wn], func=AF.Tanh,
                                                                  scale=0.5), reads=[bank_buf[bg]], writes=[buf(f"gt{m % 2}")])
                    release(bg)
                    pb = bank_buf[bp]
                    TR.op(DVE, lambda v, m=m, bp=bp: v.scalar_tensor_tensor(
                        out=banks[bp][:, :Wn], in0=gt[:, m % 2, :Wn], scalar=1.0, in1=banks[bp][:, :Wn],
                        op0=ALU.add, op1=ALU.mult), reads=[pb, buf(f"gt{m % 2}")], writes=[pb])
                    xm = buf(f"x{m}_{n}")
                    TR.op(DVE, lambda v, m=m, bp=bp: v.scalar_tensor_tensor(
                        out=x[:, m, cs], in0=banks[bp][:, :Wn], scalar=0.5, in1=x[:, m, cs],
                        op0=ALU.mult, op1=ALU.add), reads=[pb, xm], writes=[xm])
                    release(bp)
                    if very_last:
                        for mq in ([m - 1] if m >= 1 else []) + ([7] if m == 7 else []):
                            TR.op(ACT, lambda a, mq=mq: a.activation(out=h[:, g % 2, mq, :Wn], in_=x[:, mq, cs], func=AF.Square),
                                  reads=[buf(f"x{mq}_{n}")], writes=[buf(f"h{g % 2}")])
                    yield
                if last and l + 1 < L:
                    load_wple(l + 1)
                if very_last:
                    bS = alloc_bank()
                    for c in range(8):
                        mm(bS, 0, Wn, ones[:], h[:, g % 2, c, :Wn], c == 0, c == 7, [buf("ones"), buf(f"h{g % 2}")], c == 7)
                    rstd_in_psum(bS, Wn)
                    yv = yT.rearrange("(c p) t -> p c t", p=128)
                    for c in range(8):
                        TR.op(DVE, lambda v, c=c: v.scalar_tensor_tensor(
                            out=x[:, c, cs], in0=x[:, c, cs], scalar=pcol(P_FG + c), in1=banks[bS][:, :Wn],
                            op0=ALU.mult, op1=ALU.mult), reads=[buf(f"x{c}_{n}"), bank_buf[bS], buf("prm")], writes=[buf(f"x{c}_{n}")])
                        TR.dma(SP, yv[:, c, On:On + Wn], x[:, c, On:On + Wn], "out_q", reads=[buf(f"x{c}_{n}")], group=out_grp)
                    release(bS)
                    yield
                elif l == L - 1:
                    for c in range(8):
                        TR.op(ACT, lambda a, c=c: a.activation(out=xb[:, c, :Wn], in_=x[:, c, cs], func=AF.Square),
                              reads=[buf(f"x{c}_{n}")], writes=[buf(f"xb{c}")])
                        if c == 3:
                            yield
                    yield
                    yield
                    bS = alloc_bank()
                    for c in range(8):
                        mm(bS, 0, Wn, ones[:], xb[:, c, :Wn], c == 0, c == 7, [buf("ones"), buf(f"xb{c}")], c == 7)
                    yield
                    yield
                    norm_sqrt(bS, n)
                    yield
                    norm_recip(n)
                    yield
                    norm_apply(l, n, bS, 0, True)
                    TR.dma(SP, yT.rearrange("(c p) t -> p c t", p=128)[:, :, On:On + Wn], x[:, :, On:On + Wn], "out_q",
                           reads=[buf(f"x{c}_{n}") for c in range(8)], group=out_grp)
                    yield

            def step(gen):
                try:
                    next(gen)
                    return False
                except StopIteration:
                    return True

            def merge(A, B, At, Bt):
                MMC["A"] = 0
                MMC["B"] = 0
                a_done = A is None
                b_done = B is None
                while not a_done:
                    CUR[0] = "A"
                    a_done = step(A)
                    if not b_done and MMC["B"] / Bt < B_RATIO * MMC["A"] / At:
                        CUR[0] = "B"
                        b_done = step(B)
                while not b_done:
                    CUR[0] = "B"
                    b_done = step(B)

            norm_full(0, 0, 0)
            if OPT_BUILD == 'mixed':
                build_d31(0, 0, DVE)
                build_d31(0, 1, ACT)
                build_d31(0, 2, DVE)
            elif OPT_BUILD == 'dve':
                build_d31(0, 0, DVE)
            else:
                for j_ in range(3):
                    build_d31(0, j_, POOL)
            prevB = None
            G = L * NT
            for g in range(G):
                l, n = divmod(g, NT)
                At = 338 + (124 if n == NT - 1 else 0) + (24 if n == 0 else 0)
                Bt = 144 + (8 if (prevB is not None and (g - 1) // NT == L - 1) else 0)
                merge(A_tile(l, n), prevB, At, Bt)
                prevB = B_tile(l, n)
                if n == NT - 1:
                    TR.dma(SP, o_conv_t[l].rearrange("(j p) c -> p j c", p=128), ubt[:, l], "out_q",
                           reads=[buf(f"ubt{l}")], group=out_grp)
                    TR.dma(SP, o_pool_t[l].rearrange("(j p) c -> p j c", p=128), vt[:, l], "out_q",
                           reads=[buf(f"vt{l}")], group=out_grp)
                    TR.dma(SP, o_sconv_t[l].rearrange("(j p) c -> p j c", p=128), uct[:, l], "out_q",
                           reads=[buf(f"uct{l}")], group=out_grp)
            merge(None, prevB, 1, 152)
            SP.prog.append(lambda hh, s_=TR.sems["out_q"], v_=TR.semval["out_q"]: hh.wait_ge(s_, v_))

        plan()
        check_deadlock([SP, POOL, ACT, DVE, PE])
        block = es.enter_context(nc.Block())

        def run(eng):
            def _f(hh):
                for f_ in eng.prog:
                    f_(hh)
            return _f

        block.sync(run(SP))
        block.gpsimd(run(POOL))
        block.scalar(run(ACT))
        block.vector(run(DVE))
        block.tensor(run(PE))
        print("prog sizes:", {e.name: len(e.prog) for e in (SP, POOL, ACT, DVE, PE)}, "sbuf left", nc.sbuf_bytes_remaining)
    return nc


_NC_CACHE = {}


def _get_nc():
    if "nc" not in _NC_CACHE:
        _NC_CACHE["nc"] = build_nc()
    return _NC_CACHE["nc"]


def kernel(x_prompt, x_sample, state_pool, state_conv, state_sconv, p_prompt, p_sample,
           norm_g, w_in, w_pool_mix, pool_scale, conv_b_w, conv_b_b, ln_b_g, ln_b_b,
           sconv_w, w_out, w_ple, w_ple_gate, final_norm_g):
    f = lambda a: np.ascontiguousarray(np.asarray(a, dtype=np.float32))
    x_prompt, x_sample, state_pool, state_conv, state_sconv = map(f, (x_prompt, x_sample, state_pool, state_conv, state_sconv))
    p_prompt, p_sample = f(p_prompt), f(p_sample)
    norm_g, w_in, w_pool_mix, pool_scale, conv_b_w, conv_b_b = map(f, (norm_g, w_in, w_pool_mix, pool_scale, conv_b_w, conv_b_b))
    ln_b_g, ln_b_b, sconv_w, w_out, w_ple, w_ple_gate, final_norm_g = map(
        f, (ln_b_g, ln_b_b, sconv_w, w_out, w_ple, w_ple_gate, final_norm_g))

    prm = np.zeros((128, NPRM), np.float32)
    prm[:, P_NG:P_NG + 16] = norm_g.reshape(L, 8, 128).transpose(2, 0, 1).reshape(128, 16)
    prm[:, P_FG:P_FG + 8] = final_norm_g.reshape(8, 128).T
    prm[:, P_PS:P_PS + 4] = pool_scale.reshape(L, 2, 128).transpose(2, 0, 1).reshape(128, 4)
    prm[:, P_CW:P_CW + 186] = conv_b_w.reshape(L, 31, 3, 128).transpose(3, 0, 2, 1).reshape(128, 186)
    prm[:, P_CB:P_CB + 6] = conv_b_b.reshape(L, 3, 128).transpose(2, 0, 1).reshape(128, 6)
    prm[:, P_LG:P_LG + 6] = ln_b_g.reshape(L, 3, 128).transpose(2, 0, 1).reshape(128, 6)
    prm[:, P_LB:P_LB + 6] = ln_b_b.reshape(L, 3, 128).transpose(2, 0, 1).reshape(128, 6)
    prm[:, P_SW:P_SW + 18] = sconv_w.reshape(L, 3, 3, 128).transpose(3, 0, 2, 1).reshape(128, 18)

    in_maps = []
    for c in range(NCORES):
        s0, s1 = c * NS, (c + 1) * NS
        xT = np.concatenate([x_prompt[c].T, x_sample[s0:s1, 0, :].T], axis=1)
        peT = np.stack([np.concatenate([p_prompt[l, c].T, p_sample[l, s0:s1, 0, :].T], axis=1) for l in range(L)])
        in_maps.append({
            "xT": f(xT), "peT": f(peT), "w_in": w_in, "w_out": w_out, "w_ple": w_ple, "w_gate": w_ple_gate,
            "w_pool": w_pool_mix, "prm": prm,
            "st_pool_f": f(state_pool[:, s0:s1].transpose(0, 3, 2, 1)),
            "st_conv_f": f(state_conv[:, s0:s1].transpose(0, 3, 2, 1)),
            "st_sconv_f": f(state_sconv[:, s0:s1].transpose(0, 3, 2, 1)),
            "st_pool_n": f(state_pool[:, s0:s1]), "st_conv_n": f(state_conv[:, s0:s1]),
            "st_sconv_n": f(state_sconv[:, s0:s1]),
        })
    nc = _get_nc()
    res = run_bass_kernel_spmd(nc, in_maps, core_ids=list(range(NCORES)))
    B_, DS = x_prompt.shape[0], x_sample.shape[0]
    y_prompt = np.empty((B_, SEQ, D), np.float32)
    y_sample = np.empty((DS, 1, D), np.float32)
    pool_p = np.empty((L, B_, 15, W_A), np.float32)
    pool_s = np.empty((L, DS, 15, W_A), np.float32)
    conv_p = np.empty((L, B_, 30, W_B), np.float32)
    conv_s = np.empty((L, DS, 30, W_B), np.float32)
    sconv_p = np.empty((L, B_, 2, W_C), np.float32)
    sconv_s = np.empty((L, DS, 2, W_C), np.float32)
    for c in range(NCORES):
        r = res.results[c]
        s0, s1 = c * NS, (c + 1) * NS
        yT = np.asarray(r["yT"])
        y_prompt[c] = yT[:, :SEQ].T
        y_sample[s0:s1, 0, :] = yT[:, SEQ:].T
        pt, ct, st = np.asarray(r["o_pool_t"]), np.asarray(r["o_conv_t"]), np.asarray(r["o_sconv_t"])
        pool_p[:, c] = pt[:, :, 0:15].transpose(0, 2, 1)
        pool_s[:, s0:s1, 0:14] = np.asarray(r["o_pool_old"])
        pool_s[:, s0:s1, 14] = pt[:, :, 15:31].transpose(0, 2, 1)
        conv_p[:, c] = ct[:, :, 0:30].transpose(0, 2, 1)
        conv_s[:, s0:s1, 0:29] = np.asarray(r["o_conv_old"])
        conv_s[:, s0:s1, 29] = ct[:, :, 30:46].transpose(0, 2, 1)
        sconv_p[:, c] = st[:, :, 0:2].transpose(0, 2, 1)
        sconv_s[:, s0:s1, 0:1] = np.asarray(r["o_sconv_old"])
        sconv_s[:, s0:s1, 1] = st[:, :, 2:18].transpose(0, 2, 1)
    return (y_prompt, y_sample, pool_p, pool_s, conv_p, conv_s, sconv_p, sconv_s)
```

```python
import contextlib
import numpy as np
import concourse.bass as bass
import concourse.mybir as mybir
from concourse.bass_utils import run_bass_kernel_spmd

F32 = mybir.dt.float32
BF16 = mybir.dt.bfloat16
AF = mybir.ActivationFunctionType
ALU = mybir.AluOpType

NCORES = 8
D = 1024
L = 2
SEQ = 2048
NS = 16
TT = SEQ + NS
TWS = [296] * 6 + [288]
NT = len(TWS)
TOFF = [sum(TWS[:i]) for i in range(NT)]
TWMAX = 296
W_A, W_B, W_C = 256, 384, 384
C_VA, C_ZA, C_AB, C_GB, C_ZB, C_XC, C_BC, C_CC, C_ZC = 0, 256, 512, 896, 1280, 1664, 2048, 2432, 2816
IN_COLS = 3200
EPS = 1e-6
WIN_GROUPS = [(C_GB, C_GB + 384), (C_AB, C_AB + 384), (C_VA, C_VA + 512), (C_XC, C_XC + 384), (C_CC, C_CC + 384),
              (C_ZC, C_ZC + 384), (C_BC, C_BC + 384), (C_ZB, C_ZB + 384)]
P_NG = 0
P_FG = P_NG + 16
P_PS = P_FG + 8
P_CW = P_PS + 4
P_CB = P_CW + 186
P_LG = P_CB + 6
P_LB = P_LG + 6
P_SW = P_LB + 6
NPRM = P_SW + 18
RING = 4
WARM_MM = 130
LAST_BLK = {}
for _i, (_g, _c) in enumerate([(0, C_GB), (0, C_GB + 128), (0, C_GB + 256), (1, C_AB), (1, C_AB + 128), (1, C_AB + 256),
                               (3, C_XC), (3, C_XC + 128), (3, C_XC + 256), (4, C_CC), (4, C_CC + 128), (4, C_CC + 256),
                               (2, C_VA), (2, C_VA + 128), (2, C_VA + 256), (2, C_VA + 384)]):
    LAST_BLK[_i] = (_g, _c)
RECIP_MODE = 0
B_RATIO = 1.0
OPT_BUILD = 'dve'
OPT_SKIPSELF = True


class Ev:
    __slots__ = ("sem", "val", "snap")

    def __init__(self, sem, val, snap):
        self.sem, self.val, self.snap = sem, val, snap


class Buf:
    __slots__ = ("name", "writer", "readers")

    def __init__(self, name):
        self.name, self.writer, self.readers = name, None, []


class Eng:
    def __init__(self, name, h, semname, is_pe=False):
        self.name, self.h, self.semname, self.is_pe = name, h, semname, is_pe
        self.count = 0
        self.seen = {}
        self._snap = None
        self.prog = []
        self.abs = []

    def snap(self):
        if self._snap is None:
            self._snap = dict(self.seen)
        return self._snap


class Tracker:
    def __init__(self):
        self.sems = {}
        self.semval = {}

    def add_sem(self, name, handle):
        self.sems[name] = handle
        self.semval[name] = 0

    def _waits(self, eng, raw, oth, skip_self=False):
        need = []
        for ev in raw:
            if ev.sem == eng.semname and eng.is_pe:
                continue
            need.append(ev)
        for ev in oth:
            if ev.sem == eng.semname and (eng.is_pe or skip_self):
                continue
            need.append(ev)
        need.sort(key=lambda e: -e.val)
        for ev in need:
            if eng.seen.get(ev.sem, 0) >= ev.val:
                continue
            eng.prog.append(lambda hh, s_=self.sems[ev.sem], v_=ev.val: hh.wait_ge(s_, v_))
            eng.abs.append(("wait", ev.sem, ev))
            eng.seen[ev.sem] = ev.val
            if ev.snap:
                for k, v in ev.snap.items():
                    if eng.seen.get(k, 0) < v:
                        eng.seen[k] = v
            eng._snap = None

    def op(self, eng, fn, reads=(), writes=(), final=True, skip_self=False):
        raw = [b.writer for b in reads if b.writer is not None]
        oth = []
        for b in reads:
            if b.name.startswith("bank"):
                oth.extend(r for r in b.readers if r.sem != eng.semname)
        for b in writes:
            if b.writer is not None:
                oth.append(b.writer)
            oth.extend(b.readers)
        self._waits(eng, raw, oth, skip_self and OPT_SKIPSELF)
        if final:
            eng.count += 1
            eng.prog.append(lambda hh, fn=fn, s_=self.sems[eng.semname]: fn(hh).then_inc(s_, 1))
            eng.abs.append(("inc", eng.semname, 1))
            val = eng.count
        else:
            eng.prog.append(lambda hh, fn=fn: fn(hh))
            val = eng.count + 1
        ev = Ev(eng.semname, val, eng.snap())
        for b in reads:
            b.readers.append(ev)
        for b in writes:
            b.writer = ev
            b.readers = []
        return None

    def dma(self, q, out_ap, in_ap, semname, reads=(), writes=(), group=None):
        raw = [b.writer for b in reads if b.writer is not None]
        oth = []
        for b in writes:
            if b.writer is not None:
                oth.append(b.writer)
            oth.extend(b.readers)
        self._waits(q, raw, oth)
        self.semval[semname] += 16
        q.prog.append(lambda hh, o_=out_ap, i_=in_ap, s_=self.sems[semname]: hh.dma_start(out=o_, in_=i_).then_inc(s_, 16))
        q.abs.append(("inc", semname, 16))
        ev = Ev(semname, self.semval[semname], q.snap())
        if group is not None:
            group.append(ev)
        for b in reads:
            b.readers.append(ev)
        for b in writes:
            b.writer = ev
            b.readers = []
        return ev

    def close_group(self, group, semname):
        tot = self.semval[semname]
        for ev in group:
            ev.val = tot


def check_deadlock(engs):
    val = {}
    pc = {e.name: 0 for e in engs}
    progress = True
    while progress:
        progress = False
        for e in engs:
            while pc[e.name] < len(e.abs):
                kind, sem, a = e.abs[pc[e.name]]
                if kind == "wait":
                    if val.get(sem, 0) >= a.val:
                        pc[e.name] += 1
                        progress = True
                    else:
                        break
                else:
                    val[sem] = val.get(sem, 0) + a
                    pc[e.name] += 1
                    progress = True
    stuck = {e.name: (pc[e.name], len(e.abs), e.abs[pc[e.name]][1], e.abs[pc[e.name]][2].val, val.get(e.abs[pc[e.name]][1], 0))
             for e in engs if pc[e.name] < len(e.abs)}
    assert not stuck, f"DEADLOCK in sync plan: {stuck}"


def build_nc():
    nc = bass.Bass("TRN2", target_bir_lowering=False)
    dt = nc.dram_tensor
    xT = dt("xT", [D, TT], F32, kind="ExternalInput").ap()
    peT = dt("peT", [L, 256, TT], F32, kind="ExternalInput").ap()
    w_in = dt("w_in", [L, D, IN_COLS], F32, kind="ExternalInput").ap()
    w_out = dt("w_out", [L, D, D], F32, kind="ExternalInput").ap()
    w_ple = dt("w_ple", [L, 256, D], F32, kind="ExternalInput").ap()
    w_gate = dt("w_gate", [L, D, D], F32, kind="ExternalInput").ap()
    w_pool = dt("w_pool", [L, 4, 64, 64], F32, kind="ExternalInput").ap()
    prm_d = dt("prm", [128, NPRM], F32, kind="ExternalInput").ap()
    st_pool_f = dt("st_pool_f", [L, 256, 15, NS], F32, kind="ExternalInput").ap()
    st_conv_f = dt("st_conv_f", [L, 384, 30, NS], F32, kind="ExternalInput").ap()
    st_sconv_f = dt("st_sconv_f", [L, 384, 2, NS], F32, kind="ExternalInput").ap()
    st_pool_n = dt("st_pool_n", [L, NS, 15, 256], F32, kind="ExternalInput").ap()
    st_conv_n = dt("st_conv_n", [L, NS, 30, 384], F32, kind="ExternalInput").ap()
    st_sconv_n = dt("st_sconv_n", [L, NS, 2, 384], F32, kind="ExternalInput").ap()
    yT = dt("yT", [D, TT], F32, kind="ExternalOutput").ap()
    o_pool_t = dt("o_pool_t", [L, 256, 31], F32, kind="ExternalOutput").ap()
    o_conv_t = dt("o_conv_t", [L, 384, 46], F32, kind="ExternalOutput").ap()
    o_sconv_t = dt("o_sconv_t", [L, 384, 18], F32, kind="ExternalOutput").ap()
    o_pool_old = dt("o_pool_old", [L, NS, 14, 256], F32, kind="ExternalOutput").ap()
    o_conv_old = dt("o_conv_old", [L, NS, 29, 384], F32, kind="ExternalOutput").ap()
    o_sconv_old = dt("o_sconv_old", [L, NS, 1, 384], F32, kind="ExternalOutput").ap()

    with contextlib.ExitStack() as es:
        def sb(name, shape, dtype):
            return es.enter_context(nc.sbuf_tensor(name, shape, dtype))

        x = sb("x", [128, 8, TT], F32)
        win = sb("win", [128, 8, IN_COLS], BF16)
        ring = sb("ring", [128, RING, 8, 128], BF16)
        wple = sb("wple", [128, 2, D], BF16)
        d31 = sb("d31", [128, 93, 128], BF16)
        d3 = sb("d3", [128, 9, 128], BF16)
        pm = sb("pm", [128, 6, 128], BF16)
        ones = sb("ones", [128, 128], BF16)
        ident = sb("ident", [128, 128], BF16)
        prm = sb("prm_s", [128, NPRM], F32)
        epsT = sb("epsT", [128, 1], F32)
        stage = sb("stage", [128, L, 2, 64], F32)
        rtab = sb("rtab", [128, 2, 15], F32)
        t15 = sb("t15", [128, 2, 15], F32)
        h = sb("h", [128, 2, 8, TWMAX], BF16)
        xb = sb("xb", [128, 8, TWMAX], BF16)
        ycat = sb("ycat", [128, 8, TWMAX], BF16)
        pe = sb("pe", [128, 2, 2, TWMAX], BF16)
        ub = sb("ub", [128, 3, 30 + TWMAX], BF16)
        ucb = sb("ucb", [128, 3, 2 + TWMAX], BF16)
        vb = sb("vb", [128, 2, 16 + TWMAX], BF16)
        ubs = sb("ubs", [128, 3, 31, NS], BF16)
        zsb = sb("zsb", [128, 2, 16, NS], BF16)
        scb = sb("scb", [128, 3, 3, NS], BF16)
        ubt = sb("ubt", [128, L, 3, 46], F32)
        vt = sb("vt", [128, L, 2, 31], F32)
        uct = sb("uct", [128, L, 3, 18], F32)
        sq = sb("sq", [128, 2, TWMAX], BF16)
        sg = sb("sg", [128, TWMAX], F32)
        cb = sb("cb", [128, 3, TWMAX], F32)
        cbb = sb("cbb", [128, 2, TWMAX], BF16)
        csq = sb("csq", [128, 2, TWMAX], BF16)
        msq = sb("msq", [128, TWMAX], F32)
        sa = sb("sa", [128, TWMAX], F32)
        mneg = sb("mneg", [128, TWMAX], F32)
        rsn = sb("rsn", [128, TWMAX], F32)
        xc = sb("xc", [128, TWMAX], F32)
        szc = sb("szc", [128, TWMAX], F32)
        tt = sb("tt", [128, TWMAX], F32)
        gt = sb("gt", [128, 2, TWMAX], F32)
        banks = [es.enter_context(nc.psum_tensor(f"ps{i}", [128, 512], F32)) for i in range(8)]

        TR = Tracker()

        def sem(name):
            s = es.enter_context(nc.semaphore(name))
            TR.add_sem(name, s)
            return s

        for nm in ["pe_s", "act_s", "dve_s", "pool_s", "prm_q", "out_q", "st_q", "wple_q", "setup_q"]:
            sem(nm)
        for n in range(NT):
            sem(f"x{n}")
        for i_ in range(3):
            sem(f"x0p{i_}")
        for g in range(len(WIN_GROUPS)):
            sem(f"win{g}")
        for s in range(RING):
            sem(f"ring{s}")
        for s in range(2):
            sem(f"pe{s}")

        PE = Eng("pe", nc.tensor, "pe_s", is_pe=True)
        ACT = Eng("act", nc.scalar, "act_s")
        DVE = Eng("dve", nc.vector, "dve_s")
        POOL = Eng("pool", nc.gpsimd, "pool_s")
        SP = Eng("sp", nc.sync, None)

        B = {}

        def buf(name):
            if name not in B:
                B[name] = Buf(name)
            return B[name]

        bank_buf = [Buf(f"bank{i}") for i in range(8)]
        bank_free_order = [0] * 8
        bank_busy = [False] * 8
        order = [0]

        def alloc_bank():
            best = None
            for i in range(8):
                if not bank_busy[i] and (best is None or bank_free_order[i] < bank_free_order[best]):
                    best = i
            assert best is not None, "out of PSUM banks"
            bank_busy[best] = True
            return best

        def release(i):
            order[0] += 1
            bank_busy[i] = False
            bank_free_order[i] = order[0]

        MMC = {"A": 0, "B": 0}
        CUR = ["A"]

        def mm(bi, c0, c1, lhsT, rhs, start, stop, reads, final):
            MMC[CUR[0]] += 1
            TR.op(PE, lambda t: t.matmul(banks[bi][:, c0:c1], lhsT=lhsT, rhs=rhs, start=start, stop=stop),
                  reads=reads, writes=[bank_buf[bi]], final=final)

        def pcol(i):
            return prm[:, i:i + 1]

        blocks = []
        for l in range(L):
            for n in range(NT):
                if l == L - 1 and n == NT - 1:
                    continue
                for m in range(8):
                    blocks.append((l, "o", m))
                for m in range(8):
                    blocks.append((l, "g", m))
        ring_issued = [0]

        def ring_issue():
            i = ring_issued[0]
            if i >= len(blocks):
                return
            l, kind, m = blocks[i]
            src = (w_out if kind == "o" else w_gate)[l].rearrange("(k p) n -> p k n", p=128)[:, :, m * 128:(m + 1) * 128]
            s = i % RING
            TR.dma(POOL, ring[:, s], src, f"ring{s}", writes=[buf(f"ring{s}")])
            ring_issued[0] += 1

        ring_used = [0]

        def ring_next():
            i = ring_used[0]
            assert i < ring_issued[0]
            ring_used[0] += 1
            return i % RING

        def plan():
            prm_grp = []
            TR.dma(SP, prm[:], prm_d, "prm_q", writes=[buf("prm")], group=prm_grp)
            TR.dma(SP, stage[:], w_pool.rearrange("l (j hh) c d -> (hh c) l j d", hh=2), "prm_q",
                   writes=[buf("stage")], group=prm_grp)
            TR.close_group(prm_grp, "prm_q")
            def load_x(n, gate=()):
                TR.dma(SP, x[:, :, TOFF[n]:TOFF[n] + TWS[n]],
                       xT.rearrange("(c p) t -> p c t", p=128)[:, :, TOFF[n]:TOFF[n] + TWS[n]], f"x{n}",
                       reads=list(gate), writes=[buf(f"x{c}_{n}") for c in range(8)])

            xv0 = xT.rearrange("(c p) t -> p c t", p=128)
            for i_ in range(4):
                TR.dma(SP, x[:, 2 * i_:2 * i_ + 2, 0:TWS[0]], xv0[:, 2 * i_:2 * i_ + 2, 0:TWS[0]],
                       "x0" if i_ == 0 else f"x0p{i_ - 1}", writes=[buf(f"x{c}_0") for c in (2 * i_, 2 * i_ + 1)])
            out_grp = []
            for l in range(L):
                TR.dma(SP, o_pool_old[l], st_pool_n[l, :, 1:15, :], "out_q", group=out_grp)
                TR.dma(SP, o_conv_old[l], st_conv_n[l, :, 1:30, :], "out_q", group=out_grp)
                TR.dma(SP, o_sconv_old[l], st_sconv_n[l, :, 1:2, :], "out_q", group=out_grp)

            def load_win(l):
                for g, (c0, c1) in enumerate(WIN_GROUPS):
                    TR.dma(POOL, win[:, :, c0:c1], w_in[l].rearrange("(k p) n -> p k n", p=128)[:, :, c0:c1],
                           f"win{g}", writes=[buf(f"win{g}")])

            def load_win_group(l, g, gate=()):
                c0, c1 = WIN_GROUPS[g]
                TR.dma(POOL, win[:, :, c0:c1], w_in[l].rearrange("(k p) n -> p k n", p=128)[:, :, c0:c1],
                       f"win{g}", reads=list(gate), writes=[buf(f"win{g}")])

            def load_last_blocks(group):
                grp = []
                first = True
                for i_, (g_, c_) in LAST_BLK.items():
                    if g_ != group:
                        continue
                    kind, m_ = ("o", i_) if i_ < 8 else ("g", i_ - 8)
                    src = (w_out if kind == "o" else w_gate)[L - 1].rearrange("(k p) n -> p k n", p=128)[:, :, m_ * 128:(m_ + 1) * 128]
                    wr = [buf(f"lb{i_}")] + ([buf(f"win{g_}")] if first else [])
                    TR.dma(POOL, win[:, :, c_:c_ + 128], src, f"win{g_}", writes=wr, group=grp)
                    first = False
                TR.close_group(grp, f"win{group}")

            def load_wple(l):
                TR.dma(POOL, wple[:], w_ple[l].rearrange("(k p) n -> p k n", p=128), "wple_q", writes=[buf("wple")])

            def load_states(l):
                grp = []
                TR.dma(POOL, ubs[:, :, 0:30, :], st_conv_f[l].rearrange("(j p) r s -> p j r s", p=128), "st_q",
                       writes=[buf("ubs")], group=grp)
                TR.dma(POOL, zsb[:, :, 0:15, :], st_pool_f[l].rearrange("(j p) r s -> p j r s", p=128), "st_q",
                       writes=[buf("zsb")], group=grp)
                TR.dma(POOL, scb[:, :, 0:2, :], st_sconv_f[l].rearrange("(j p) r s -> p j r s", p=128), "st_q",
                       writes=[buf("scb")], group=grp)
                TR.close_group(grp, "st_q")

            def build_d31(l, j, eng=None):
                eng = eng or POOL
                for k in range(31):
                    idx = j * 31 + k
                    sc = pcol(P_CW + l * 93 + idx)
                    rd, wr = [buf("ident"), buf("prm")], [buf(f"d31_{j}")]
                    if eng is POOL:
                        TR.op(POOL, lambda g, idx=idx, sc=sc: g.tensor_scalar(
                            out=d31[:, idx, :], in0=ident[:], scalar1=sc, scalar2=0.0,
                            op0=ALU.mult, op1=ALU.add), reads=rd, writes=wr, skip_self=True)
                    elif eng is DVE:
                        TR.op(DVE, lambda v, idx=idx, sc=sc: v.tensor_scalar(
                            out=d31[:, idx, :], in0=ident[:], scalar1=sc, scalar2=None, op0=ALU.mult),
                            reads=rd, writes=wr, skip_self=True)
                    else:
                        TR.op(ACT, lambda a, idx=idx, sc=sc: a.activation(
                            out=d31[:, idx, :], in_=ident[:], func=AF.Copy, scale=sc), reads=rd, writes=wr, skip_self=True)

            def build_d3(l):
                for j in range(3):
                    for k in range(3):
                        idx = j * 3 + k
                        TR.op(POOL, lambda g, idx=idx, l=l: g.tensor_scalar(
                            out=d3[:, idx, :], in0=ident[:], scalar1=pcol(P_SW + l * 9 + idx), scalar2=0.0,
                            op0=ALU.mult, op1=ALU.add), reads=[buf("ident"), buf("prm")], writes=[buf(f"d3_{j}")], skip_self=True)

            def build_pm(l):
                for j in range(2):
                    w_lo, w_hi = ((2, 4), (8, 16))[j]
                    specs = [(0, 64, j * 3 + 0, 1.0 / w_lo), (64, 128, j * 3 + 0, 1.0 / w_hi),
                             (64, 128, j * 3 + 1, 1.0 / w_hi), (0, 64, j * 3 + 2, -1.0), (64, 128, j * 3 + 2, -1.0)]
                    for (p0, p1, mi, sc) in specs:
                        TR.op(POOL, lambda g, p0=p0, p1=p1, mi=mi, sc=sc, l=l, j=j: g.tensor_scalar(
                            out=pm[p0:p1, mi, p0:p1], in0=stage[p0:p1, l, j, :], scalar1=float(sc), scalar2=0.0,
                            op0=ALU.mult, op1=ALU.add), reads=[buf("stage")], writes=[buf(f"pm_{j}")])

            def build_diags(l):
                build_d3(l)

            TR.op(DVE, lambda v: v.memset(ones[:], 1.0), writes=[buf("ones")])
            TR.op(DVE, lambda v: v.memset(epsT[:], EPS), writes=[buf("eps")])
            TR.op(ACT, lambda a: a.activation(out=t15[:, 0, 0:1], in_=epsT[:], func=AF.Sqrt), reads=[buf("eps")], writes=[buf("t15_0")])
            load_win_group(0, 0, gate=[buf("x1_0")])
            load_win_group(0, 1)
            TR.op(POOL, lambda g: g.affine_select(out=ident[:], in_=ones[:], pattern=[[1, 128]], compare_op=ALU.is_equal,
                                                  fill=0.0, base=0, channel_multiplier=-1),
                  reads=[buf("ones")], writes=[buf("ident")])
            for g_ in range(2, 5):
                load_win_group(0, g_)
            load_x(1, gate=[buf("win3")])
            build_d31(0, 1, POOL)
            build_d31(0, 2, POOL)
            TR.op(POOL, lambda g: g.memset(pm[:], 0.0), writes=[buf("pm_0"), buf("pm_1")])
            build_pm(0)
            for g_ in range(5, len(WIN_GROUPS)):
                load_win_group(0, g_)
            load_wple(0)
            for _i in range(RING):
                ring_issue()
            TR.op(POOL, lambda g: g.memset(rtab[:], 1.0), writes=[buf("rtab")])
            for j in range(2):
                for hh in range(2):
                    w = ((2, 4), (8, 16))[j][hh]
                    for t in range(w - 1):
                        TR.op(POOL, lambda g, j=j, hh=hh, t=t, w=w: g.memset(rtab[hh * 64:(hh + 1) * 64, j, t:t + 1],
                                                                          float(w) / (t + 1)), writes=[buf("rtab")])
            build_diags(0)

            def geom(n):
                Wn = TWS[n]
                last = (n == NT - 1)
                PW = Wn - NS if last else Wn
                return Wn, TOFF[n], PW, last

            def win_group_of(col):
                for g, (c0, c1) in enumerate(WIN_GROUPS):
                    if c0 <= col < c1:
                        return g
                raise AssertionError

            def inproj(col0, Wn, hb):
                bi = alloc_bank()
                g = win_group_of(col0)
                for k in range(8):
                    mm(bi, 0, Wn, win[:, k, col0:col0 + 128], h[:, hb, k, :Wn], k == 0, k == 7,
                       [buf(f"win{g}"), buf(f"h{hb}")], k == 7)
                return bi

            def recip(bi, Wn):
                bb = bank_buf[bi]
                if RECIP_MODE == 0:
                    TR.op(DVE, lambda v: v.reciprocal(out=banks[bi][:, :Wn], in_=banks[bi][:, :Wn]), reads=[bb], writes=[bb])
                else:
                    TR.op(DVE, lambda v: v.reciprocal_approx_accurate(out=banks[bi][:, :Wn], in_=banks[bi][:, :Wn],
                                                                      scratch=rscr[:, :Wn]),
                          reads=[bb], writes=[bb, buf("rscr")])

            def norm_head(n, scr, scr_bufs):
                Wn, On, PW, last = geom(n)
                cs = slice(On, On + Wn)
                for c in range(8):
                    TR.op(ACT, lambda a, c=c: a.activation(out=scr(c)[:, :Wn], in_=x[:, c, cs], func=AF.Square),
                          reads=[buf(f"x{c}_{n}")], writes=[scr_bufs[c]])
                bS = alloc_bank()
                for c in range(8):
                    mm(bS, 0, Wn, ones[:], scr(c)[:, :Wn], c == 0, c == 7, [buf("ones"), scr_bufs[c]], c == 7)
                return bS

            def norm_sqrt(bS, n):
                Wn = TWS[n]
                bb = bank_buf[bS]
                TR.op(ACT, lambda a: a.activation(out=rsn[:, :Wn], in_=banks[bS][:, :Wn], func=AF.Sqrt,
                                                  bias=epsT[:], scale=1.0 / D), reads=[bb, buf("eps")], writes=[buf("rsn")])
                release(bS)

            def norm_recip(n):
                Wn = TWS[n]
                TR.op(DVE, lambda v: v.reciprocal(out=rsn[:, :Wn], in_=rsn[:, :Wn]), reads=[buf("rsn")], writes=[buf("rsn")])

            def norm_apply(l, n, bS, hb, final_norm):
                Wn, On, PW, last = geom(n)
                cs = slice(On, On + Wn)
                for c in range(8):
                    if final_norm:
                        TR.op(DVE, lambda v, c=c: v.scalar_tensor_tensor(
                            out=x[:, c, cs], in0=x[:, c, cs], scalar=pcol(P_FG + c), in1=rsn[:, :Wn],
                            op0=ALU.mult, op1=ALU.mult), reads=[buf(f"x{c}_{n}"), buf("rsn"), buf("prm")], writes=[buf(f"x{c}_{n}")])
                    else:
                        TR.op(DVE, lambda v, c=c: v.scalar_tensor_tensor(
                            out=h[:, hb, c, :Wn], in0=x[:, c, cs], scalar=pcol(P_NG + l * 8 + c), in1=rsn[:, :Wn],
                            op0=ALU.mult, op1=ALU.mult), reads=[buf(f"x{c}_{n}"), buf("rsn"), buf("prm")], writes=[buf(f"h{hb}")])

            def rstd_in_psum(bS, Wn):
                bb = bank_buf[bS]
                TR.op(ACT, lambda a: a.activation(out=banks[bS][:, :Wn], in_=banks[bS][:, :Wn], func=AF.Sqrt,
                                                  bias=epsT[:], scale=1.0 / D), reads=[bb, buf("eps")], writes=[bb])
                TR.op(DVE, lambda v: v.reciprocal(out=banks[bS][:, :Wn], in_=banks[bS][:, :Wn]), reads=[bb], writes=[bb])

            def norm_full(l, n, hb, final_norm=False):
                Wn, On, PW, last = geom(n)
                cs = slice(On, On + Wn)
                bS = norm_head(n, lambda c: h[:, hb, c, :], [buf(f"h{hb}")] * 8)
                bw = alloc_bank()
                for i_ in range(WARM_MM):
                    mm(bw, 0, 128, ones[:], ident[:], True, True, [buf("ones"), buf("ident")], i_ == WARM_MM - 1)
                release(bw)
                rstd_in_psum(bS, Wn)
                for c in range(8):
                    TR.op(DVE, lambda v, c=c: v.scalar_tensor_tensor(
                        out=h[:, hb, c, :Wn], in0=x[:, c, cs], scalar=pcol(P_NG + l * 8 + c), in1=banks[bS][:, :Wn],
                        op0=ALU.mult, op1=ALU.mult), reads=[buf(f"x{c}_{n}"), bank_buf[bS], buf("prm")], writes=[buf(f"h{hb}")])
                release(bS)

            def A_tile(l, n):
                g = l * NT + n
                hb = g % 2
                Wn, On, PW, last = geom(n)
                Wp = TWMAX
                boundary = last and (l + 1 < L)
                slot = g % 2
                if l == 0 and n + 2 < NT:
                    load_x(n + 2, gate=[buf(f"h{hb}")])
                TR.dma(POOL, pe[:, slot, :, :Wn], peT[l].rearrange("(k p) t -> p k t", p=128)[:, :, On:On + Wn],
                       f"pe{slot}", writes=[buf(f"pe{slot}")])
                for j in range(3):
                    bg = inproj(C_GB + 128 * j, Wn, hb)
                    TR.op(ACT, lambda a, bg=bg: a.activation(out=sg[:, :Wn], in_=banks[bg][:, :Wn], func=AF.Tanh, scale=0.5),
                          reads=[bank_buf[bg]], writes=[buf("sg")])
                    release(bg)
                    TR.op(DVE, lambda v: v.tensor_scalar(out=sg[:, :Wn], in0=sg[:, :Wn], scalar1=0.5, scalar2=0.5,
                                                         op0=ALU.mult, op1=ALU.add), reads=[buf("sg")], writes=[buf("sg")])
                    yield
                    ba = inproj(C_AB + 128 * j, Wn, hb)
                    if n == 0:
                        TR.op(DVE, lambda v, j=j: v.memset(ub[:, j, 0:30], 0.0), writes=[buf(f"ub{j}")])
                    else:
                        TR.op(DVE, lambda v, j=j: v.tensor_copy(out=ub[:, j, 0:30], in_=ub[:, j, Wp:Wp + 30]),
                              reads=[buf(f"ub{j}")], writes=[buf(f"ub{j}")])
                    TR.op(DVE, lambda v, j=j, ba=ba: v.tensor_tensor(out=ub[:, j, 30:30 + Wn], in0=banks[ba][:, :Wn],
                                                                     in1=sg[:, :Wn], op=ALU.mult),
                          reads=[bank_buf[ba], buf("sg")], writes=[buf(f"ub{j}")])
                    if last:
                        TR.op(DVE, lambda v, j=j, ba=ba: v.tensor_tensor(out=ubt[:, l, j, :], in0=banks[ba][:, PW - 30:Wn],
                                                                         in1=sg[:, PW - 30:Wn], op=ALU.mult),
                              reads=[bank_buf[ba], buf("sg")], writes=[buf(f"ubt{l}")])
                        TR.op(DVE, lambda v, j=j: v.tensor_copy(out=ubs[:, j, 30, :], in_=ub[:, j, 30 + PW:30 + Wn]),
                              reads=[buf(f"ub{j}")], writes=[buf("ubs")])
                    release(ba)
                    yield
                if boundary:
                    load_win_group(l + 1, 0)
                    load_win_group(l + 1, 1)
                if l == L - 1 and last:
                    load_last_blocks(0)
                    load_last_blocks(1)
                if l >= 1 and n == 0:
                    load_win_group(l, 5)
                    load_win_group(l, 6)
                    load_win_group(l, 7)
                def c_part1(j):
                    bx = inproj(C_XC + 128 * j, Wn, hb)
                    TR.op(ACT, lambda a, bx=bx: a.activation(out=xc[:, :Wn], in_=banks[bx][:, :Wn], func=AF.Copy),
                          reads=[bank_buf[bx]], writes=[buf("xc")])
                    release(bx)
                    yield 0
                    bc = inproj(C_CC + 128 * j, Wn, hb)
                    if n == 0:
                        TR.op(DVE, lambda v, j=j: v.memset(ucb[:, j, 0:2], 0.0), writes=[buf(f"ucb{j}")])
                    else:
                        TR.op(DVE, lambda v, j=j: v.tensor_copy(out=ucb[:, j, 0:2], in_=ucb[:, j, Wp:Wp + 2]),
                              reads=[buf(f"ucb{j}")], writes=[buf(f"ucb{j}")])
                    TR.op(DVE, lambda v, j=j, bc=bc: v.tensor_tensor(out=ucb[:, j, 2:2 + Wn], in0=banks[bc][:, :Wn],
                                                                     in1=xc[:, :Wn], op=ALU.mult),
                          reads=[bank_buf[bc], buf("xc")], writes=[buf(f"ucb{j}")])
                    if last:
                        TR.op(DVE, lambda v, j=j, bc=bc: v.tensor_tensor(out=uct[:, l, j, :], in0=banks[bc][:, PW - 2:Wn],
                                                                         in1=xc[:, PW - 2:Wn], op=ALU.mult),
                              reads=[bank_buf[bc], buf("xc")], writes=[buf(f"uct{l}")])
                        TR.op(DVE, lambda v, j=j: v.tensor_copy(out=scb[:, j, 2, :], in_=ucb[:, j, 2 + PW:2 + Wn]),
                              reads=[buf(f"ucb{j}")], writes=[buf("scb")])
                    release(bc)
                    yield 1

                def va_chunk(j):
                    bv = inproj(C_VA + 128 * j, Wn, hb)
                    if n == 0:
                        TR.op(DVE, lambda v, j=j: v.memset(vb[:, j, 0:16], 0.0), writes=[buf(f"vb{j}")])
                    else:
                        TR.op(DVE, lambda v, j=j: v.tensor_copy(out=vb[:, j, 1:16], in_=vb[:, j, Wp + 1:Wp + 16]),
                              reads=[buf(f"vb{j}")], writes=[buf(f"vb{j}")])
                    TR.op(ACT, lambda a, j=j, bv=bv: a.activation(out=vb[:, j, 16:16 + Wn], in_=banks[bv][:, :Wn], func=AF.Copy),
                          reads=[bank_buf[bv]], writes=[buf(f"vb{j}")])
                    if last:
                        TR.op(ACT, lambda a, j=j, bv=bv: a.activation(out=vt[:, l, j, :], in_=banks[bv][:, PW - 15:Wn], func=AF.Copy),
                              reads=[bank_buf[bv]], writes=[buf(f"vt{l}")])
                        TR.op(ACT, lambda a, j=j, bv=bv: a.activation(out=zsb[:, j, 15, :], in_=banks[bv][:, PW:Wn], func=AF.Copy),
                              reads=[bank_buf[bv]], writes=[buf("zsb")])
                    release(bv)

                bsx = {}

                def stats(j):
                    if j == 0:
                        bsx[1] = alloc_bank()
                        bsx[2] = alloc_bank()
                    mm(bsx[1], 0, Wn, ones[:], cbb[:, j % 2, :Wn], j == 0, j == 2, [buf("ones"), buf(f"cbb{j % 2}")], True)
                    mm(bsx[2], 0, Wn, ones[:], csq[:, j % 2, :Wn], j == 0, j == 2, [buf("ones"), buf(f"csq{j % 2}")], True)

                for j in range(3):
                    bcv = alloc_bank()
                    for k in range(31):
                        mm(bcv, 0, PW, d31[:, j * 31 + k, :], ub[:, j, k:k + PW], k == 0, k == 30,
                           [buf(f"d31_{j}"), buf(f"ub{j}")], (k == 30 and not last))
                    if last:
                        for k in range(31):
                            mm(bcv, PW, Wn, d31[:, j * 31 + k, :], ubs[:, j, k, :], k == 0, k == 30,
                               [buf(f"d31_{j}"), buf("ubs")], k == 30)
                    bias = pcol(P_CB + l * 3 + j)
                    TR.op(ACT, lambda a, j=j, bcv=bcv, bias=bias: a.activation(
                        out=cb[:, j, :Wn], in_=banks[bcv][:, :Wn], func=AF.Identity, bias=bias),
                        reads=[bank_buf[bcv], buf("prm")], writes=[buf(f"cb{j}")])
                    TR.op(ACT, lambda a, j=j, bcv=bcv, bias=bias: a.activation(
                        out=cbb[:, j % 2, :Wn], in_=banks[bcv][:, :Wn], func=AF.Identity, bias=bias),
                        reads=[bank_buf[bcv], buf("prm")], writes=[buf(f"cbb{j % 2}")])
                    TR.op(ACT, lambda a, j=j, bcv=bcv, bias=bias: a.activation(
                        out=csq[:, j % 2, :Wn], in_=banks[bcv][:, :Wn], func=AF.Square, bias=bias),
                        reads=[bank_buf[bcv], buf("prm")], writes=[buf(f"csq{j % 2}")])
                    release(bcv)
                    if boundary:
                        build_d31(l + 1, j, DVE)
                    if j >= 1:
                        stats(j - 1)
                    yield

                nxt = (l, n + 1) if n + 1 < NT else ((l + 1, 0) if l + 1 < L else None)
                st = {}
                chain = []

                hn = (g + 1) % 2
                has_n = nxt is not None

                def c_mneg():
                    bs1 = bsx[1]
                    TR.op(DVE, lambda v: v.tensor_scalar(out=mneg[:, :Wn], in0=banks[bs1][:, :Wn], scalar1=-1.0 / W_B,
                                                         scalar2=None, op0=ALU.mult), reads=[bank_buf[bs1]], writes=[buf("mneg")])
                    release(bs1)

                def c_msq():
                    TR.op(ACT, lambda a: a.activation(out=msq[:, :Wn], in_=mneg[:, :Wn], func=AF.Square),
                          reads=[buf("mneg")], writes=[buf("msq")])

                def c_var():
                    bs2 = bsx[2]
                    TR.op(DVE, lambda v: v.scalar_tensor_tensor(out=msq[:, :Wn], in0=banks[bs2][:, :Wn], scalar=1.0 / W_B,
                                                                in1=msq[:, :Wn], op0=ALU.mult, op1=ALU.subtract),
                          reads=[bank_buf[bs2], buf("msq")], writes=[buf("msq")])
                    release(bs2)

                def c_nsq(ca=0, cz=8):
                    if has_n:
                        n2 = nxt[1]
                        W2, O2 = TWS[n2], TOFF[n2]
                        for c in range(ca, cz):
                            TR.op(ACT, lambda a, c=c: a.activation(out=h[:, hn, c, :W2], in_=x[:, c, O2:O2 + W2], func=AF.Square),
                                  reads=[buf(f"x{c}_{n2}")], writes=[buf(f"h{hn}")])

                def c_nmm():
                    if has_n:
                        n2 = nxt[1]
                        W2 = TWS[n2]
                        bS = alloc_bank()
                        for c in range(8):
                            mm(bS, 0, W2, ones[:], h[:, hn, c, :W2], c == 0, c == 7, [buf("ones"), buf(f"h{hn}")], c == 7)
                        st["bS"] = bS

                def c_sqrts():
                    TR.op(ACT, lambda a: a.activation(out=msq[:, :Wn], in_=msq[:, :Wn], func=AF.Relu),
                          reads=[buf("msq")], writes=[buf("msq")])
                    TR.op(ACT, lambda a: a.activation(out=msq[:, :Wn], in_=msq[:, :Wn], func=AF.Sqrt, bias=epsT[:],
                                                      scale=1.0), reads=[buf("msq"), buf("eps")], writes=[buf("msq")])
                    if has_n:
                        norm_sqrt(st["bS"], nxt[1])

                def c_rec_ln():
                    TR.op(DVE, lambda v: v.reciprocal(out=msq[:, :Wn], in_=msq[:, :Wn]), reads=[buf("msq")], writes=[buf("msq")])

                def c_rec_n():
                    if has_n:
                        norm_recip(nxt[1])

                def c_ln_dve(j):
                    cj = buf(f"cb{j}")
                    TR.op(DVE, lambda v: v.tensor_tensor(out=cb[:, j, :Wn], in0=cb[:, j, :Wn], in1=mneg[:, :Wn],
                                                         op=ALU.add), reads=[cj, buf("mneg")], writes=[cj])
                    TR.op(DVE, lambda v: v.tensor_tensor(out=cb[:, j, :Wn], in0=cb[:, j, :Wn], in1=msq[:, :Wn],
                                                         op=ALU.mult), reads=[cj, buf("msq")], writes=[cj])

                def c_ln_act(j):
                    cj = buf(f"cb{j}")
                    TR.op(ACT, lambda a: a.activation(out=cb[:, j, :Wn], in_=cb[:, j, :Wn], func=AF.Silu,
                                                      bias=pcol(P_LB + l * 3 + j), scale=pcol(P_LG + l * 3 + j)),
                          reads=[cj, buf("prm")], writes=[cj])

                def c_napply(c0, c1):
                    if has_n:
                        n2 = nxt[1]
                        W2, O2 = TWS[n2], TOFF[n2]
                        for c in range(c0, c1):
                            TR.op(DVE, lambda v, c=c: v.scalar_tensor_tensor(
                                out=h[:, hn, c, :W2], in0=x[:, c, O2:O2 + W2], scalar=pcol(P_NG + nxt[0] * 8 + c), in1=rsn[:, :W2],
                                op0=ALU.mult, op1=ALU.mult), reads=[buf(f"x{c}_{n2}"), buf("rsn"), buf("prm")], writes=[buf(f"h{hn}")])

                chain += [lambda: (c_mneg(), c_nsq(0, 3)), lambda: (c_msq(), c_nsq(3, 6)), lambda: (c_var(), c_nsq(6, 8)),
                          lambda: None, c_nmm, lambda: None, c_sqrts, c_rec_ln, c_rec_n,
                          lambda: c_ln_dve(0), lambda: c_ln_dve(1), lambda: (c_ln_dve(2), c_ln_act(0)),
                          lambda: (c_napply(0, 2), c_ln_act(1)), lambda: (c_napply(2, 4), c_ln_act(2)),
                          lambda: c_napply(4, 6), lambda: c_napply(6, 8)]

                def drip(k=1):
                    for _ in range(k):
                        if chain:
                            chain.pop(0)()

                va_chunk(0)
                stats(2)
                yield
                va_chunk(1)
                drip()
                yield
                for j in range(3):
                    for _half in c_part1(j):
                        drip()
                        yield
                if boundary:
                    load_win_group(l + 1, 3)
                    load_win_group(l + 1, 4)
                if l >= 1 and n == 0:
                    build_pm(l)
                    build_d3(l)
                if n == 1:
                    load_states(l)
                if l == L - 1 and last:
                    load_last_blocks(3)
                    load_last_blocks(4)
                bfix = None
                for j in range(2):
                    bz = inproj(C_ZA + 128 * j, Wn, hb)
                    TR.op(ACT, lambda a, j=j, bz=bz: a.activation(out=sa[:, :Wn], in_=banks[bz][:, :Wn], func=AF.Silu),
                          reads=[bank_buf[bz]], writes=[buf("sa")])
                    release(bz)
                    drip()
                    yield
                    w_lo, w_hi = ((2, 4), (8, 16))[j]
                    bm = alloc_bank()
                    rd = [buf(f"pm_{j}"), buf(f"vb{j}")]
                    for d in range(w_hi):
                        mi = j * 3 + (0 if d < w_lo else 1)
                        mm(bm, 0, PW, pm[:, mi, :], vb[:, j, 16 - d:16 - d + PW], d == 0, False, rd, False)
                    mm(bm, 0, PW, pm[:, j * 3 + 2, :], vb[:, j, 16:16 + PW], False, True, rd, not last)
                    if last:
                        rd2 = [buf(f"pm_{j}"), buf("zsb")]
                        for d in range(w_hi):
                            mi = j * 3 + (0 if d < w_lo else 1)
                            mm(bm, PW, Wn, pm[:, mi, :], zsb[:, j, 15 - d, :], d == 0, False, rd2, False)
                        mm(bm, PW, Wn, pm[:, j * 3 + 2, :], zsb[:, j, 15, :], False, True, rd2, True)
                    if n == 0:
                        if bfix is None:
                            bfix = alloc_bank()
                        c0 = j * 64
                        for d in range(w_hi):
                            mi = j * 3 + (0 if d < w_lo else 1)
                            mm(bfix, c0, c0 + 15, pm[:, mi, :], vb[:, j, 16 - d:31 - d], d == 0, d == w_hi - 1, rd, False)
                        mm(bfix, c0 + 16, c0 + 31, pm[:, j * 3 + 2, :], vb[:, j, 16:31], True, True, rd, True)
                    TR.op(DVE, lambda v, j=j, bm=bm: v.scalar_tensor_tensor(
                        out=ycat[:, j, :Wn], in0=banks[bm][:, :Wn], scalar=pcol(P_PS + l * 2 + j), in1=sa[:, :Wn],
                        op0=ALU.mult, op1=ALU.mult), reads=[bank_buf[bm], buf("sa"), buf("prm")], writes=[buf(f"ycat{j}")])
                    release(bm)
                    if n == 0:
                        c0 = j * 64
                        fb = bank_buf[bfix]
                        TR.op(DVE, lambda v, j=j, c0=c0: v.tensor_tensor(out=t15[:, j, :], in0=banks[bfix][:, c0:c0 + 15],
                                                                         in1=rtab[:, j, :], op=ALU.mult),
                              reads=[fb, buf("rtab")], writes=[buf(f"t15_{j}")])
                        TR.op(DVE, lambda v, j=j, c0=c0: v.tensor_tensor(out=t15[:, j, :], in0=t15[:, j, :],
                                                                         in1=banks[bfix][:, c0 + 16:c0 + 31], op=ALU.add),
                              reads=[fb, buf(f"t15_{j}")], writes=[buf(f"t15_{j}")])
                        TR.op(DVE, lambda v, j=j: v.scalar_tensor_tensor(
                            out=ycat[:, j, 0:15], in0=t15[:, j, :], scalar=pcol(P_PS + l * 2 + j), in1=sa[:, 0:15],
                            op0=ALU.mult, op1=ALU.mult), reads=[buf(f"t15_{j}"), buf("sa"), buf("prm")],
                            writes=[buf(f"ycat{j}")])
                    drip()
                    yield
                if n == 0:
                    release(bfix)
                if boundary:
                    load_win_group(l + 1, 2)
                if l == L - 1 and last:
                    load_last_blocks(2)
                for j in range(3):
                    bzc = inproj(C_ZC + 128 * j, Wn, hb)
                    TR.op(ACT, lambda a, bzc=bzc: a.activation(out=szc[:, :Wn], in_=banks[bzc][:, :Wn], func=AF.Silu),
                          reads=[bank_buf[bzc]], writes=[buf("szc")])
                    release(bzc)
                    drip()
                    yield
                    bbc = inproj(C_BC + 128 * j, Wn, hb)
                    TR.op(DVE, lambda v, bbc=bbc: v.tensor_tensor(out=tt[:, :Wn], in0=banks[bbc][:, :Wn], in1=szc[:, :Wn],
                                                                  op=ALU.mult), reads=[bank_buf[bbc], buf("szc")], writes=[buf("tt")])
                    release(bbc)
                    bsc = alloc_bank()
                    for k in range(3):
                        mm(bsc, 0, PW, d3[:, j * 3 + k, :], ucb[:, j, k:k + PW], k == 0, k == 2,
                           [buf(f"d3_{j}"), buf(f"ucb{j}")], (k == 2 and not last))
                    if last:
                        for k in range(3):
                            mm(bsc, PW, Wn, d3[:, j * 3 + k, :], scb[:, j, k, :], k == 0, k == 2,
                               [buf(f"d3_{j}"), buf("scb")], k == 2)
                    TR.op(DVE, lambda v, j=j, bsc=bsc: v.tensor_tensor(out=ycat[:, 5 + j, :Wn], in0=banks[bsc][:, :Wn],
                                                                       in1=tt[:, :Wn], op=ALU.mult),
                          reads=[bank_buf[bsc], buf("tt")], writes=[buf(f"ycat{5 + j}")])
                    release(bsc)
                    drip()
                    yield
                if boundary:
                    pass
                drip(len(chain))
                for j in range(3):
                    bz = inproj(C_ZB + 128 * j, Wn, hb)
                    zb_ = bank_buf[bz]
                    TR.op(ACT, lambda a, bz=bz: a.activation(out=banks[bz][:, :Wn], in_=banks[bz][:, :Wn], func=AF.Silu),
                          reads=[zb_], writes=[zb_])
                    TR.op(DVE, lambda v, j=j, bz=bz: v.tensor_tensor(out=ycat[:, 2 + j, :Wn], in0=cb[:, j, :Wn],
                                                                     in1=banks[bz][:, :Wn], op=ALU.mult),
                          reads=[zb_, buf(f"cb{j}")], writes=[buf(f"ycat{2 + j}")])
                    release(bz)
                    yield

            def B_tile(l, n):
                g = l * NT + n
                Wn, On, PW, last = geom(n)
                cs = slice(On, On + Wn)
                slot = g % 2
                ycs = [buf(f"ycat{k}") for k in range(8)]
                very_last = (l == L - 1 and n == NT - 1)
                for m in range(8):
                    bo = alloc_bank()
                    KORD = [0, 1, 5, 6, 7, 2, 3, 4]
                    if very_last:
                        _g, _c = LAST_BLK[m]
                        for ki, k in enumerate(KORD):
                            mm(bo, 0, Wn, win[:, k, _c:_c + 128], ycat[:, k, :Wn], ki == 0, ki == 7, [buf(f"lb{m}"), ycs[k]], ki == 7)
                    else:
                        rs = ring_next()
                        for ki, k in enumerate(KORD):
                            mm(bo, 0, Wn, ring[:, rs, k, :], ycat[:, k, :Wn], ki == 0, ki == 7, [buf(f"ring{rs}"), ycs[k]], ki == 7)
                        ring_issue()
                    xm = buf(f"x{m}_{n}")
                    TR.op(DVE, lambda v, m=m, bo=bo: v.tensor_tensor(out=x[:, m, cs], in0=banks[bo][:, :Wn], in1=x[:, m, cs],
                                                                     op=ALU.add), reads=[bank_buf[bo], xm], writes=[xm])
                    release(bo)
                    TR.op(ACT, lambda a, m=m: a.activation(out=xb[:, m, :Wn], in_=x[:, m, cs], func=AF.Copy),
                          reads=[xm], writes=[buf(f"xb{m}")])
                    yield
                for m in range(8):
                    bg = alloc_bank()
                    if very_last:
                        _g, _c = LAST_BLK[8 + m]
                        for k in range(8):
                            mm(bg, 0, Wn, win[:, k, _c:_c + 128], xb[:, k, :Wn], k == 0, k == 7, [buf(f"lb{8 + m}"), buf(f"xb{k}")], k == 7)
                    else:
                        rs = ring_next()
                        for k in range(8):
                            mm(bg, 0, Wn, ring[:, rs, k, :], xb[:, k, :Wn], k == 0, k == 7, [buf(f"ring{rs}"), buf(f"xb{k}")], k == 7)
                        ring_issue()
                    bp = alloc_bank()
                    for k in range(2):
                        mm(bp, 0, Wn, wple[:, k, m * 128:(m + 1) * 128], pe[:, slot, k, :Wn], k == 0, k == 1,
                           [buf("wple"), buf(f"pe{slot}")], k == 1)
                    TR.op(ACT, lambda a, m=m, bg=bg: a.activation(out=gt[:, m % 2, :Wn], in_=banks[bg][:, :Wn], func=AF.Tanh,
                                                                  scale=0.5), reads=[bank_buf[bg]], writes=[buf(f"gt{m % 2}")])
                    release(bg)
                    pb = bank_buf[bp]
                    TR.op(DVE, lambda v, m=m, bp=bp: v.scalar_tensor_tensor(
                        out=banks[bp][:, :Wn], in0=gt[:, m % 2, :Wn], scalar=1.0, in1=banks[bp][:, :Wn],
                        op0=ALU.add, op1=ALU.mult), reads=[pb, buf(f"gt{m % 2}")], writes=[pb])
                    xm = buf(f"x{m}_{n}")
                    TR.op(DVE, lambda v, m=m, bp=bp: v.scalar_tensor_tensor(
                        out=x[:, m, cs], in0=banks[bp][:, :Wn], scalar=0.5, in1=x[:, m, cs],
                        op0=ALU.mult, op1=ALU.add), reads=[pb, xm], writes=[xm])
                    release(bp)
                    if very_last:
                        for mq in ([m - 1] if m >= 1 else []) + ([7] if m == 7 else []):
                            TR.op(ACT, lambda a, mq=mq: a.activation(out=h[:, g % 2, mq, :Wn], in_=x[:, mq, cs], func=AF.Square),
                                  reads=[buf(f"x{mq}_{n}")], writes=[buf(f"h{g % 2}")])
                    yield
                if last and l + 1 < L:
                    load_wple(l + 1)
                if very_last:
                    bS = alloc_bank()
                    for c in range(8):
                        mm(bS, 0, Wn, ones[:], h[:, g % 2, c, :Wn], c == 0, c == 7, [buf("ones"), buf(f"h{g % 2}")], c == 7)
                    rstd_in_psum(bS, Wn)
                    yv = yT.rearrange("(c p) t -> p c t", p=128)
                    for c in range(8):
                        TR.op(DVE, lambda v, c=c: v.scalar_tensor_tensor(
                            out=x[:, c, cs], in0=x[:, c, cs], scalar=pcol(P_FG + c), in1=banks[bS][:, :Wn],
                            op0=ALU.mult, op1=ALU.mult), reads=[buf(f"x{c}_{n}"), bank_buf[bS], buf("prm")], writes=[buf(f"x{c}_{n}")])
                        TR.dma(SP, yv[:, c, On:On + Wn], x[:, c, On:On + Wn], "out_q", reads=[buf(f"x{c}_{n}")], group=out_grp)
                    release(bS)
                    yield
                elif l == L - 1:
                    for c in range(8):
                        TR.op(ACT, lambda a, c=c: a.activation(out=xb[:, c, :Wn], in_=x[:, c, cs], func=AF.Square),
                              reads=[buf(f"x{c}_{n}")], writes=[buf(f"xb{c}")])
                        if c == 3:
                            yield
                    yield
                    yield
                    bS = alloc_bank()
                    for c in range(8):
                        mm(bS, 0, Wn, ones[:], xb[:, c, :Wn], c == 0, c == 7, [buf("ones"), buf(f"xb{c}")], c == 7)
                    yield
                    yield
                    norm_sqrt(bS, n)
                    yield
                    norm_recip(n)
                    yield
                    norm_apply(l, n, bS, 0, True)
                    TR.dma(SP, yT.rearrange("(c p) t -> p c t", p=128)[:, :, On:On + Wn], x[:, :, On:On + Wn], "out_q",
                           reads=[buf(f"x{c}_{n}") for c in range(8)], group=out_grp)
                    yield

            def step(gen):
                try:
                    next(gen)
                    return False
                except StopIteration:
                    return True

            def merge(A, B, At, Bt):
                MMC["A"] = 0
                MMC["B"] = 0
                a_done = A is None
                b_done = B is None
                while not a_done:
                    CUR[0] = "A"
                    a_done = step(A)
                    if not b_done and MMC["B"] / Bt < B_RATIO * MMC["A"] / At:
                        CUR[0] = "B"
                        b_done = step(B)
                while not b_done:
                    CUR[0] = "B"
                    b_done = step(B)

            norm_full(0, 0, 0)
            if OPT_BUILD == 'mixed':
                build_d31(0, 0, DVE)
                build_d31(0, 1, ACT)
                build_d31(0, 2, DVE)
            elif OPT_BUILD == 'dve':
                build_d31(0, 0, DVE)
            else:
                for j_ in range(3):
                    build_d31(0, j_, POOL)
            prevB = None
            G = L * NT
            for g in range(G):
                l, n = divmod(g, NT)
                At = 338 + (124 if n == NT - 1 else 0) + (24 if n == 0 else 0)
                Bt = 144 + (8 if (prevB is not None and (g - 1) // NT == L - 1) else 0)
                merge(A_tile(l, n), prevB, At, Bt)
                prevB = B_tile(l, n)
                if n == NT - 1:
                    TR.dma(SP, o_conv_t[l].rearrange("(j p) c -> p j c", p=128), ubt[:, l], "out_q",
                           reads=[buf(f"ubt{l}")], group=out_grp)
                    TR.dma(SP, o_pool_t[l].rearrange("(j p) c -> p j c", p=128), vt[:, l], "out_q",
                           reads=[buf(f"vt{l}")], group=out_grp)
                    TR.dma(SP, o_sconv_t[l].rearrange("(j p) c -> p j c", p=128), uct[:, l], "out_q",
                           reads=[buf(f"uct{l}")], group=out_grp)
            merge(None, prevB, 1, 152)
            SP.prog.append(lambda hh, s_=TR.sems["out_q"], v_=TR.semval["out_q"]: hh.wait_ge(s_, v_))

        plan()
        check_deadlock([SP, POOL, ACT, DVE, PE])
        block = es.enter_context(nc.Block())

        def run(eng):
            def _f(hh):
                for f_ in eng.prog:
                    f_(hh)
            return _f

        block.sync(run(SP))
        block.gpsimd(run(POOL))
        block.scalar(run(ACT))
        block.vector(run(DVE))
        block.tensor(run(PE))
        print("prog sizes:", {e.name: len(e.prog) for e in (SP, POOL, ACT, DVE, PE)}, "sbuf left", nc.sbuf_bytes_remaining)
    return nc


_NC_CACHE = {}


def _get_nc():
    if "nc" not in _NC_CACHE:
        _NC_CACHE["nc"] = build_nc()
    return _NC_CACHE["nc"]


def kernel(x_prompt, x_sample, state_pool, state_conv, state_sconv, p_prompt, p_sample,
           norm_g, w_in, w_pool_mix, pool_scale, conv_b_w, conv_b_b, ln_b_g, ln_b_b,
           sconv_w, w_out, w_ple, w_ple_gate, final_norm_g):
    f = lambda a: np.ascontiguousarray(np.asarray(a, dtype=np.float32))
    x_prompt, x_sample, state_pool, state_conv, state_sconv = map(f, (x_prompt, x_sample, state_pool, state_conv, state_sconv))
    p_prompt, p_sample = f(p_prompt), f(p_sample)
    norm_g, w_in, w_pool_mix, pool_scale, conv_b_w, conv_b_b = map(f, (norm_g, w_in, w_pool_mix, pool_scale, conv_b_w, conv_b_b))
    ln_b_g, ln_b_b, sconv_w, w_out, w_ple, w_ple_gate, final_norm_g = map(
        f, (ln_b_g, ln_b_b, sconv_w, w_out, w_ple, w_ple_gate, final_norm_g))

    prm = np.zeros((128, NPRM), np.float32)
    prm[:, P_NG:P_NG + 16] = norm_g.reshape(L, 8, 128).transpose(2, 0, 1).reshape(128, 16)
    prm[:, P_FG:P_FG + 8] = final_norm_g.reshape(8, 128).T
    prm[:, P_PS:P_PS + 4] = pool_scale.reshape(L, 2, 128).transpose(2, 0, 1).reshape(128, 4)
    prm[:, P_CW:P_CW + 186] = conv_b_w.reshape(L, 31, 3, 128).transpose(3, 0, 2, 1).reshape(128, 186)
    prm[:, P_CB:P_CB + 6] = conv_b_b.reshape(L, 3, 128).transpose(2, 0, 1).reshape(128, 6)
    prm[:, P_LG:P_LG + 6] = ln_b_g.reshape(L, 3, 128).transpose(2, 0, 1).reshape(128, 6)
    prm[:, P_LB:P_LB + 6] = ln_b_b.reshape(L, 3, 128).transpose(2, 0, 1).reshape(128, 6)
    prm[:, P_SW:P_SW + 18] = sconv_w.reshape(L, 3, 3, 128).transpose(3, 0, 2, 1).reshape(128, 18)

    in_maps = []
    for c in range(NCORES):
        s0, s1 = c * NS, (c + 1) * NS
        xT = np.concatenate([x_prompt[c].T, x_sample[s0:s1, 0, :].T], axis=1)
        peT = np.stack([np.concatenate([p_prompt[l, c].T, p_sample[l, s0:s1, 0, :].T], axis=1) for l in range(L)])
        in_maps.append({
            "xT": f(xT), "peT": f(peT), "w_in": w_in, "w_out": w_out, "w_ple": w_ple, "w_gate": w_ple_gate,
            "w_pool": w_pool_mix, "prm": prm,
            "st_pool_f": f(state_pool[:, s0:s1].transpose(0, 3, 2, 1)),
            "st_conv_f": f(state_conv[:, s0:s1].transpose(0, 3, 2, 1)),
            "st_sconv_f": f(state_sconv[:, s0:s1].transpose(0, 3, 2, 1)),
            "st_pool_n": f(state_pool[:, s0:s1]), "st_conv_n": f(state_conv[:, s0:s1]),
            "st_sconv_n": f(state_sconv[:, s0:s1]),
        })
    nc = _get_nc()
    res = run_bass_kernel_spmd(nc, in_maps, core_ids=list(range(NCORES)))
    B_, DS = x_prompt.shape[0], x_sample.shape[0]
    y_prompt = np.empty((B_, SEQ, D), np.float32)
    y_sample = np.empty((DS, 1, D), np.float32)
    pool_p = np.empty((L, B_, 15, W_A), np.float32)
    pool_s = np.empty((L, DS, 15, W_A), np.float32)
    conv_p = np.empty((L, B_, 30, W_B), np.float32)
    conv_s = np.empty((L, DS, 30, W_B), np.float32)
    sconv_p = np.empty((L, B_, 2, W_C), np.float32)
    sconv_s = np.empty((L, DS, 2, W_C), np.float32)
    for c in range(NCORES):
        r = res.results[c]
        s0, s1 = c * NS, (c + 1) * NS
        yT = np.asarray(r["yT"])
        y_prompt[c] = yT[:, :SEQ].T
        y_sample[s0:s1, 0, :] = yT[:, SEQ:].T
        pt, ct, st = np.asarray(r["o_pool_t"]), np.asarray(r["o_conv_t"]), np.asarray(r["o_sconv_t"])
        pool_p[:, c] = pt[:, :, 0:15].transpose(0, 2, 1)
        pool_s[:, s0:s1, 0:14] = np.asarray(r["o_pool_old"])
        pool_s[:, s0:s1, 14] = pt[:, :, 15:31].transpose(0, 2, 1)
        conv_p[:, c] = ct[:, :, 0:30].transpose(0, 2, 1)
        conv_s[:, s0:s1, 0:29] = np.asarray(r["o_conv_old"])
        conv_s[:, s0:s1, 29] = ct[:, :, 30:46].transpose(0, 2, 1)
        sconv_p[:, c] = st[:, :, 0:2].transpose(0, 2, 1)
        sconv_s[:, s0:s1, 0:1] = np.asarray(r["o_sconv_old"])
        sconv_s[:, s0:s1, 1] = st[:, :, 2:18].transpose(0, 2, 1)
    return (y_prompt, y_sample, pool_p, pool_s, conv_p, conv_s, sconv_p, sconv_s)
```
